# Optimizing a Trainium2 kernel written in Bass

```python
import math
import jax, jax.numpy as jnp
from jax import lax
import numpy as np

D_MODEL = 1024
BATCH = 8
SEQ = 2048
DEPTH = 1

N_MEM = 256
EPS = 1e-6
MLA_HEADS = 8
MLA_NOPE = 64
MLA_ROPE = 32
MLA_V = 64
MLA_Q_RANK = 384
MLA_KV_RANK = 256
ROPE_THETA = 10000.0
Q_BLOCK = 128
GLA_HEADS = 4
GLA_DK = 64
GLA_DV = 128
GLA_GATE_RANK = 16
GLA_TAU = 16.0
GLA_CHUNK = 64
X_HEADS = 4
X_DH = 128
D_FF = 2816
N_BRANCH = 3
IN_WIDTHS = (MLA_Q_RANK, MLA_KV_RANK, MLA_ROPE,
             GLA_HEADS * GLA_DK, GLA_HEADS * GLA_DK, GLA_HEADS * GLA_DV, GLA_GATE_RANK, GLA_HEADS * GLA_DV,
             X_HEADS * X_DH,
             N_BRANCH * D_MODEL)
D_IN = (MLA_Q_RANK + MLA_KV_RANK + MLA_ROPE + 2 * GLA_HEADS * GLA_DK + 2 * GLA_HEADS * GLA_DV
        + GLA_GATE_RANK + X_HEADS * X_DH + N_BRANCH * D_MODEL)

kernel_name = "hybrid_mla_gla_xattn_macaron"


def _rmsnorm(x, g):
    x32 = x.astype(jnp.float32)
    y = x32 * lax.rsqrt(jnp.mean(x32 * x32, axis=-1, keepdims=True) + EPS)
    return (y * g.astype(jnp.float32)).astype(x.dtype)


def _swiglu(x, wg, wu, wd):
    return (jax.nn.silu(x @ wg) * (x @ wu)) @ wd


def _rope(x, pos):
    half = x.shape[-1] // 2
    inv_freq = ROPE_THETA ** (-jnp.arange(half, dtype=jnp.float32) / half)
    ang = pos.astype(jnp.float32)[..., None] * inv_freq
    cos = jnp.cos(ang).astype(x.dtype)
    sin = jnp.sin(ang).astype(x.dtype)
    x1, x2 = x[..., :half], x[..., half:]
    return jnp.concatenate([x1 * cos - x2 * sin, x1 * sin + x2 * cos], axis=-1)


def _split_cols(z, widths):
    outs, start = [], 0
    for w in widths:
        outs.append(z[..., start:start + w])
        start += w
    return outs


def _mla(cq, ckv, krope, pos, q_norm, w_uq, kv_norm, w_ukv):
    B, S, _ = cq.shape
    H = MLA_HEADS
    q = (_rmsnorm(cq, q_norm) @ w_uq).reshape(B, S, H, MLA_NOPE + MLA_ROPE).transpose(0, 2, 1, 3)
    q = jnp.concatenate([q[..., :MLA_NOPE], _rope(q[..., MLA_NOPE:], pos[:, None, :])], axis=-1)
    kv = (_rmsnorm(ckv, kv_norm) @ w_ukv).reshape(B, S, H, MLA_NOPE + MLA_V).transpose(0, 2, 1, 3)
    k_nope, v = kv[..., :MLA_NOPE], kv[..., MLA_NOPE:]
    k_rope = jnp.broadcast_to(_rope(krope, pos)[:, None], (B, H, S, MLA_ROPE))
    k = jnp.concatenate([k_nope, k_rope], axis=-1)
    scale = 1.0 / math.sqrt(MLA_NOPE + MLA_ROPE)
    neg = jnp.finfo(jnp.float32).min
    outs = []
    for i in range(S // Q_BLOCK):
        s0, e = i * Q_BLOCK, (i + 1) * Q_BLOCK
        s = jnp.einsum('bhqd,bhkd->bhqk', q[:, :, s0:e], k[:, :, :e]).astype(jnp.float32) * scale
        mask = (s0 + jnp.arange(Q_BLOCK))[:, None] >= jnp.arange(e)[None, :]
        p = jax.nn.softmax(jnp.where(mask, s, neg), axis=-1).astype(v.dtype)
        outs.append(jnp.einsum('bhqk,bhkd->bhqd', p, v[:, :, :e]))
    o = jnp.concatenate(outs, axis=2)
    return o.transpose(0, 2, 1, 3).reshape(B, S, H * MLA_V)


def _gla(q, k, v, a_lr, r, w_a2, b_a, o_norm):
    B, S, _ = q.shape
    H, C = GLA_HEADS, GLA_CHUNK
    N = S // C
    f32 = jnp.float32

    def chunks(t, d):
        return t.reshape(B, N, C, H, d).transpose(0, 3, 1, 2, 4)

    qc = chunks(q.astype(f32), GLA_DK) * (GLA_DK ** -0.5)
    kc = chunks(k.astype(f32), GLA_DK)
    vc = chunks(v.astype(f32), GLA_DV)
    log_a = jax.nn.log_sigmoid((a_lr @ w_a2 + b_a).astype(f32)) / GLA_TAU
    bcum = jnp.cumsum(chunks(log_a, GLA_DK), axis=3)
    q_t = qc * jnp.exp(bcum)
    k_t = kc * jnp.exp(-bcum)
    att = jnp.einsum('bhncd,bhnjd->bhncj', q_t, k_t)
    tril = jnp.tril(jnp.ones((C, C), dtype=bool))
    o_intra = jnp.einsum('bhncj,bhnjv->bhncv', jnp.where(tril, att, 0.0), vc)
    b_last = bcum[:, :, :, -1:, :]
    dS = jnp.einsum('bhncd,bhncv->bhndv', kc * jnp.exp(b_last - bcum), vc)
    decay = jnp.exp(b_last[:, :, :, 0, :])

    def step(state, inp):
        dec, ds = inp
        return dec[..., None] * state + ds, state

    s0 = jnp.zeros((B, H, GLA_DK, GLA_DV), f32)
    _, s_prev = lax.scan(step, s0, (decay.transpose(2, 0, 1, 3), dS.transpose(2, 0, 1, 3, 4)))
    o_inter = jnp.einsum('bhncd,bhndv->bhncv', q_t, s_prev.transpose(1, 2, 0, 3, 4))
    o = (o_intra + o_inter).reshape(B, H, S, GLA_DV)
    o = _rmsnorm(o, o_norm)
    o = o.transpose(0, 2, 1, 3).reshape(B, S, H * GLA_DV).astype(r.dtype)
    return o * jax.nn.silu(r)


def _cross(xq, mem_n, w_kv):
    B, S, _ = xq.shape
    M = mem_n.shape[1]
    kv = (mem_n @ w_kv).reshape(B, M, 2, X_HEADS, X_DH)
    q = xq.reshape(B, S, X_HEADS, X_DH)
    s = jnp.einsum('bshd,bmhd->bhsm', q, kv[:, :, 0]).astype(jnp.float32) / math.sqrt(X_DH)
    p = jax.nn.softmax(s, axis=-1).astype(kv.dtype)
    o = jnp.einsum('bhsm,bmhd->bshd', p, kv[:, :, 1])
    return o.reshape(B, S, X_HEADS * X_DH)


def _mixer(h, mem, positions, mix_norm, mem_norm, w_in, gate_bias,
           mla_q_norm, mla_w_uq, mla_kv_norm, mla_w_ukv, mla_w_o,
           gla_w_a2, gla_b_a, gla_o_norm, gla_w_o, x_w_kv, x_w_o, w_out):
    B, S, D = h.shape
    u = _rmsnorm(h, mix_norm)
    z = u @ w_in
    cq, ckv, krope, gq, gk, gv, ga, gr, xq, gz = _split_cols(z, IN_WIDTHS)
    o_mla = _mla(cq, ckv, krope, positions, mla_q_norm, mla_w_uq, mla_kv_norm, mla_w_ukv) @ mla_w_o
    o_gla = _gla(gq, gk, gv, ga, gr, gla_w_a2, gla_b_a, gla_o_norm) @ gla_w_o
    o_x = _cross(xq, _rmsnorm(mem, mem_norm), x_w_kv) @ x_w_o
    gates = jax.nn.sigmoid(gz.reshape(B, S, N_BRANCH, D) + gate_bias)
    merged = gates[:, :, 0] * o_mla + gates[:, :, 1] * o_gla + gates[:, :, 2] * o_x
    return merged @ w_out


def setup_inputs(seed: int = 0) -> dict:
    key = jax.random.key(seed)
    ks = iter(jax.random.split(key, 40))
    L, D, F = DEPTH, D_MODEL, D_FF

    def w(shape, fan_in):
        return jax.random.normal(next(ks), shape, jnp.float32) * (fan_in ** -0.5)

    def gain(shape):
        return 1.0 + 0.05 * jax.random.normal(next(ks), shape, jnp.float32)

    def bias(shape, s=0.1):
        return s * jax.random.normal(next(ks), shape, jnp.float32)

    x = jax.random.normal(next(ks), (BATCH, SEQ, D), jnp.float32)
    mem = jax.random.normal(next(ks), (BATCH, N_MEM, D), jnp.float32)
    start = jax.random.randint(next(ks), (BATCH, 1), 0, 4096, dtype=jnp.int32)
    positions = (start + jnp.arange(SEQ, dtype=jnp.int32)[None, :]).astype(jnp.int32)
    return {
        "x": x, "mem": mem, "positions": positions,
        "ffn1_norm": gain((L, D)), "ffn1_wg": w((L, D, F), D), "ffn1_wu": w((L, D, F), D), "ffn1_wd": w((L, F, D), F),
        "mix_norm": gain((L, D)), "mem_norm": gain((L, D)),
        "w_in": w((L, D, D_IN), D), "gate_bias": bias((L, N_BRANCH, D)),
        "mla_q_norm": gain((L, MLA_Q_RANK)),
        "mla_w_uq": w((L, MLA_Q_RANK, MLA_HEADS * (MLA_NOPE + MLA_ROPE)), MLA_Q_RANK),
        "mla_kv_norm": gain((L, MLA_KV_RANK)),
        "mla_w_ukv": w((L, MLA_KV_RANK, MLA_HEADS * (MLA_NOPE + MLA_V)), MLA_KV_RANK),
        "mla_w_o": w((L, MLA_HEADS * MLA_V, D), MLA_HEADS * MLA_V),
        "gla_w_a2": w((L, GLA_GATE_RANK, GLA_HEADS * GLA_DK), GLA_GATE_RANK),
        "gla_b_a": bias((L, GLA_HEADS * GLA_DK)),
        "gla_o_norm": gain((L, GLA_DV)),
        "gla_w_o": w((L, GLA_HEADS * GLA_DV, D), GLA_HEADS * GLA_DV),
        "x_w_kv": w((L, D, 2 * X_HEADS * X_DH), D),
        "x_w_o": w((L, X_HEADS * X_DH, D), X_HEADS * X_DH),
        "w_out": w((L, D, D), D),
        "ffn2_norm": gain((L, D)), "ffn2_wg": w((L, D, F), D), "ffn2_wu": w((L, D, F), D), "ffn2_wd": w((L, F, D), F),
        "final_norm": gain((D,)),
    }


def reference(x, mem, positions, ffn1_norm, ffn1_wg, ffn1_wu, ffn1_wd, mix_norm, mem_norm, w_in, gate_bias,
              mla_q_norm, mla_w_uq, mla_kv_norm, mla_w_ukv, mla_w_o, gla_w_a2, gla_b_a, gla_o_norm, gla_w_o,
              x_w_kv, x_w_o, w_out, ffn2_norm, ffn2_wg, ffn2_wu, ffn2_wd, final_norm):
    h = x
    for l in range(DEPTH):
        h = h + 0.5 * _swiglu(_rmsnorm(h, ffn1_norm[l]), ffn1_wg[l], ffn1_wu[l], ffn1_wd[l])
        h = h + _mixer(h, mem, positions, mix_norm[l], mem_norm[l], w_in[l], gate_bias[l],
                       mla_q_norm[l], mla_w_uq[l], mla_kv_norm[l], mla_w_ukv[l], mla_w_o[l],
                       gla_w_a2[l], gla_b_a[l], gla_o_norm[l], gla_w_o[l], x_w_kv[l], x_w_o[l], w_out[l])
        h = h + 0.5 * _swiglu(_rmsnorm(h, ffn2_norm[l]), ffn2_wg[l], ffn2_wu[l], ffn2_wd[l])
    return _rmsnorm(h, final_norm)
```

```python
import math
import numpy as np
import concourse.bass as bass
import concourse.mybir as mybir
from concourse.bass_utils import run_bass_kernel_spmd

F32 = mybir.dt.float32
BF16 = mybir.dt.bfloat16
I32 = mybir.dt.int32
AF = mybir.ActivationFunctionType
ALU = mybir.AluOpType

D = 1024
S = 2048
NMEM = 256
DFF = 2816
DIN = 5808
EPS = 1e-6
TT = 512
NT = S // TT


class T:
    def __init__(self, ap, name=""):
        self.ap = ap
        self.name = name
        self.w = None
        self.r = []

    def __getitem__(self, idx):
        return R(self, self.ap[idx])

    def bf(self):
        return R(self, self.ap.bitcast(BF16))

    def i32(self):
        return R(self, self.ap.bitcast(I32))


class R:
    def __init__(self, t, ap):
        self.t = t
        self.ap = ap

    def __getitem__(self, idx):
        return R(self.t, self.ap[idx])

    def re(self, s, **kw):
        return R(self.t, self.ap.rearrange(s, **kw))


def _ap(x):
    return x.ap if isinstance(x, (T, R)) else x


def _t(x):
    if isinstance(x, T):
        return x
    if isinstance(x, R):
        return x.t
    return None


class Eng:
    def __init__(self, name, h, semidx, sem):
        self.name = name
        self.h = h
        self.semidx = semidx
        self.sem = sem
        self.count = 0
        self.known = {}
        self.ninstr = 0
        self.nwait = 0


class FW:
    def __init__(self, nc, n_dma_sems=48):
        self.nc = nc
        self.sems = []
        self._stack = []
        self.pe = self._mk("pe", nc.tensor)
        self.act = self._mk("act", nc.scalar)
        self.dve = self._mk("dve", nc.vector)
        self.pool = self._mk("pool", nc.gpsimd)
        self.sp = self._mk("sp", nc.sync)
        self.dma_sems = []
        for i in range(n_dma_sems):
            idx = self._new_sem(f"dma{i}")
            self.dma_sems.append([idx, 0])
        self.dma_rr = 0
        self.out_events = []

    def _new_sem(self, name):
        cm = self.nc.semaphore(name)
        s = cm.__enter__()
        self._stack.append(cm)
        self.sems.append(s)
        return len(self.sems) - 1

    def _mk(self, name, h):
        idx = self._new_sem("s_" + name)
        return Eng(name, h, idx, self.sems[idx])

    def close(self):
        for cm in reversed(self._stack):
            cm.__exit__(None, None, None)
        self._stack = []

    def sb(self, name, shape, dtype):
        cm = self.nc.sbuf_tensor(name, list(shape), dtype)
        h = cm.__enter__()
        self._stack.append(cm)
        return h

    def ps(self, name, shape, dtype=F32):
        cm = self.nc.psum_tensor(name, list(shape), dtype)
        h = cm.__enter__()
        self._stack.append(cm)
        return h

    def _wait(self, eng, events):
        need = {}
        for ev in events:
            if ev is None:
                continue
            s, v, _ = ev
            if v > need.get(s, 0):
                need[s] = v
        for s, v in need.items():
            if eng.known.get(s, 0) >= v:
                continue
            eng.h.wait_ge(self.sems[s], v)
            eng.known[s] = v
            eng.nwait += 1

    def _deps(self, eng, outs, ins):
        evs = []
        for x in ins:
            t = _t(x)
            if t is None or t.w is None:
                continue
            if t.w[2] == eng.name and eng.name == "pe":
                continue
            evs.append(t.w)
        for x in outs:
            t = _t(x)
            if t is None:
                continue
            if t.w is not None and t.w[2] != eng.name:
                evs.append(t.w)
            for ev in t.r:
                if ev[2] != eng.name:
                    evs.append(ev)
        return evs

    def op(self, eng, fn, outs=(), ins=(), inc=True):
        self._wait(eng, self._deps(eng, outs, ins))
        instr = fn(eng.h)
        eng.ninstr += 1
        if inc:
            instr.then_inc(eng.sem, 1)
            eng.count += 1
            ev = (eng.semidx, eng.count, eng.name)
        else:
            ev = (eng.semidx, eng.count + 1, eng.name)
        for x in ins:
            t = _t(x)
            if t is not None:
                t.r.append(ev)
                if len(t.r) > 64:
                    t.r = t.r[-48:]
        for x in outs:
            t = _t(x)
            if t is not None:
                t.w = ev
                t.r = []
        return instr

    def dma(self, q, out, in_, is_output=False, **kw):
        outs = [out] if _t(out) is not None else []
        ins = [in_] if _t(in_) is not None else []
        evs = []
        for x in ins:
            t = _t(x)
            if t.w is not None:
                evs.append(t.w)
        for x in outs:
            t = _t(x)
            if t.w is not None:
                evs.append(t.w)
            evs.extend(t.r)
        slot = self.dma_sems[self.dma_rr % len(self.dma_sems)]
        self.dma_rr += 1
        if slot[1] > 0:
            evs.append((slot[0], slot[1], "dma"))
        self._wait(q, evs)
        instr = q.h.dma_start(out=_ap(out), in_=_ap(in_), **kw)
        q.ninstr += 1
        slot[1] += 16
        instr.then_inc(self.sems[slot[0]], 16)
        ev = (slot[0], slot[1], "dma")
        for x in ins:
            _t(x).r.append(ev)
        for x in outs:
            t = _t(x)
            t.w = ev
            t.r = []
        if is_output:
            self.out_events.append(ev)
        return ev

    def barrier(self):
        engs = (self.pe, self.act, self.dve)
        for e in engs:
            evs = [(o.semidx, o.count, o.name) for o in engs if o is not e and o.count > 0]
            self._wait(e, evs)

    def finish(self):
        evs = list(self.out_events)
        for slot in self.dma_sems:
            if slot[1] > 0:
                evs.append((slot[0], slot[1], "dma"))
        self._wait(self.sp, evs)


class Pool_:
    def __init__(self, tiles):
        self.free = list(tiles)
        self.n = len(tiles)

    def get(self):
        assert self.free, "scratch pool exhausted"
        self.low = min(getattr(self, "low", 999), len(self.free) - 1)
        return self.free.pop(0)

    def put(self, *ts):
        for t in ts:
            assert t not in self.free
            self.free.append(t)


C_CQ, C_CKV, C_KR, C_GQ, C_GK, C_GV, C_GA, C_GR, C_XQ, C_GZ = 0, 384, 640, 672, 928, 1184, 1696, 1712, 2224, 2736
V_F1, V_MIX, V_F2, V_FIN, V_MEM, V_GB, V_QN, V_KVN, V_ON, V_IF, V_SG = 0, 8, 16, 24, 32, 40, 64, 67, 69, 70, 71
NV = 72


LOOKAHEAD = 3


def build_program(stop_after=None, wl_seq=None):
    nc = bass.Bass("TRN2", target_bir_lowering=False)
    wl_rec = []

    def din(name, shape, dt=F32):
        return nc.dram_tensor(name, list(shape), dt, kind="ExternalInput").ap()

    xT_d = din("xT", [D, S])
    memT_d = din("memT", [D, NMEM])
    posr_d = din("posr", [32, S], I32)
    vecs_d = din("vecs", [128, NV])
    utri_d = din("utri", [128, 128])
    sltri_d = din("sltri", [128, 128])
    ident_d = din("ident", [128, 128])
    f1g_d, f1u_d, f1d_d = din("f1g", [D, DFF]), din("f1u", [D, DFF]), din("f1d", [DFF, D])
    f2g_d, f2u_d, f2d_d = din("f2g", [D, DFF]), din("f2u", [D, DFF]), din("f2d", [DFF, D])
    win_d = din("win", [D, DIN])
    wkrot_d = din("wkrot", [D, 96])
    wuq_d = din("wuq", [384, 768])
    wuqr_d = din("wuqr", [384, 768])
    wukv_d = din("wukv", [256, 1024])
    wa2_d = din("wa2", [32, 256])
    wo_mla_d, wo_gla_d, wo_x_d = din("wo_mla", [512, D]), din("wo_gla", [512, D]), din("wo_x", [512, D])
    xwkv_d = din("xwkv", [D, D])
    wout_d = din("wout", [D, D])
    outT_d = nc.dram_tensor("outT", [D, S], F32, kind="ExternalOutput").ap()

    f = FW(nc)
    pe, act, dve, pool, sp = f.pe, f.act, f.dve, f.pool, f.sp

    hT_h = f.sb("hT", [128, 8, S], F32)
    hT = [[T(hT_h[:, c, t * TT:(t + 1) * TT], f"hT{c}_{t}") for t in range(NT)] for c in range(8)]
    B_h = f.sb("B", [128, 8 * S], BF16)
    vaug_flat = f.sb("vaug", [128, 16 * 8 * 65 + 64], BF16)
    vaug_h = vaug_flat[:, 0:16 * 8 * 65].rearrange("p (k h d) -> p k h d", k=16, h=8)
    vaug = [T(vaug_h[:, kc], f"vaug{kc}") for kc in range(16)]
    vaug_pad = T(vaug_flat[:, 16 * 8 * 65:], "vaug_pad")
    NSLOT = 6
    ring = [T(f.sb(f"ring{i}", [128, 2048], BF16)[:], f"ring{i}") for i in range(NSLOT)]
    ring_pos = [0]
    NSCR = 27
    scr = Pool_([T(f.sb(f"scr{i}", [128, 512], F32)[:], f"scr{i}") for i in range(NSCR)])
    ps_all = f.ps("ps_all", [128, 8, 512], F32)
    psum = [T(ps_all[:, i, :], f"ps{i}") for i in range(8)]
    pair_pos = [0]
    ps_pos = [0]

    ps_held = set()
    ps_last = [0] * 8
    ps_clock = [0]

    def _stamp(i):
        ps_clock[0] += 1
        ps_last[i] = ps_clock[0]

    def ps_get(hold=False):
        cand = [i for i in range(8) if i not in ps_held]
        i = min(cand, key=lambda b: ps_last[b])
        _stamp(i)
        if hold:
            ps_held.add(i)
        return psum[i]

    def ps_rel(p):
        ps_held.discard(psum.index(p))

    def ps_take(i):
        ps_held.add(i)
        _stamp(i)
        return psum[i]

    def ps_pair():
        cand = [j for j in range(4) if 2 * j not in ps_held and 2 * j + 1 not in ps_held]
        j = min(cand, key=lambda q: max(ps_last[2 * q], ps_last[2 * q + 1]))
        _stamp(2 * j)
        _stamp(2 * j + 1)
        return psum[2 * j], psum[2 * j + 1], ps_all[:, 2 * j:2 * j + 2, :]

    vecs = T(f.sb("vecs_s", [128, NV], F32)[:], "vecs")
    utri = T(f.sb("utri_s", [128, 128], F32)[:], "utri")
    sltri = T(f.sb("sltri_s", [128, 128], F32)[:], "sltri")
    utri4 = T(f.sb("utri4", [128, 4, 128], BF16)[:], "utri4")
    ident_bf = T(f.sb("ident_bf", [128, 128], BF16)[:], "ident_bf")
    negm_bf = T(f.sb("negm_bf", [128, 128], BF16)[:], "negm_bf")
    ones_bf = T(f.sb("ones_bf", [128, 128], BF16)[:], "ones_bf")
    ones_f = T(f.sb("ones_f", [128, 128], F32)[:], "ones_f")
    eps_t = T(f.sb("eps", [128, 1], F32)[:], "eps")
    wa2_t = T(f.sb("wa2_s", [32, 256], F32)[:], "wa2")
    gaug = T(f.sb("gaug", [32, TT], F32)[:], "gaug")
    gstate = T(f.sb("gstate", [128, 2, 128], F32)[:], "gstate")
    gstate_bf = [T(f.sb(f"gstate_bf{i}", [128, 128], BF16)[:], f"gstate_bf{i}") for i in range(4)]
    ktz_all = [T(f.sb(f"ktz{i}", [128, TT], BF16)[:], f"ktz{i}") for i in range(4)]
    gdec = T(f.sb("gdec", [128, 8], F32)[:], "gdec")
    kxT = T(f.sb("kxT", [128, 4, NMEM], BF16)[:], "kxT")
    vx = T(f.sb("vx", [128, 2, 512], BF16)[:], "vx")

    def mm(out, lhsT, rhs, start, stop):
        f.op(pe, lambda e: e.matmul(_ap(out), lhsT=_ap(lhsT), rhs=_ap(rhs), start=start, stop=stop),
             outs=[out], ins=[lhsT, rhs], inc=stop)

    def actf(out, in_, func, scale=1.0, bias=None):
        ins = [in_]
        kw = {}
        if bias is not None:
            kw["bias"] = _ap(bias)
            ins.append(bias)
        if isinstance(scale, (T, R)):
            ins.append(scale)
        f.op(act, lambda e: e.activation(out=_ap(out), in_=_ap(in_), func=func, scale=_ap(scale), **kw),
             outs=[out], ins=ins)

    def tt(out, a, b, op, eng=None):
        f.op(eng or dve, lambda e: e.tensor_tensor(out=_ap(out), in0=_ap(a), in1=_ap(b), op=op),
             outs=[out], ins=[a, b])

    def stt(out, in0, scalar, in1, op0, op1):
        ins = [in0, in1] + ([scalar] if isinstance(scalar, (T, R)) else [])
        f.op(dve, lambda e: e.scalar_tensor_tensor(out=_ap(out), in0=_ap(in0), scalar=_ap(scalar), in1=_ap(in1),
                                                   op0=op0, op1=op1), outs=[out], ins=ins)

    def ts(out, in0, s1, s2, op0, op1=None, eng=None):
        ins = [in0] + [s for s in (s1, s2) if isinstance(s, (T, R))]
        if op1 is None:
            f.op(eng or dve, lambda e: e.tensor_scalar(out=_ap(out), in0=_ap(in0), scalar1=_ap(s1), scalar2=None,
                                                       op0=op0), outs=[out], ins=ins)
        else:
            f.op(eng or dve, lambda e: e.tensor_scalar(out=_ap(out), in0=_ap(in0), scalar1=_ap(s1), scalar2=_ap(s2),
                                                       op0=op0, op1=op1), outs=[out], ins=ins)

    def cp(out, in_, eng=None):
        eng = eng or act
        if eng is act:
            f.op(act, lambda e: e.activation(out=_ap(out), in_=_ap(in_), func=AF.Copy), outs=[out], ins=[in_])
        else:
            f.op(eng, lambda e: e.tensor_copy(out=_ap(out), in_=_ap(in_)), outs=[out], ins=[in_])

    def memset(x, val, eng=None):
        f.op(eng or dve, lambda e: e.memset(_ap(x), val), outs=[x])

    wviews = {}
    issued = {}

    def _issue(i, name, idx):
        dram_ap = wviews[name][(slice(None),) + tuple(idx)]
        slot = ring[i % NSLOT]
        shp = dram_ap.shape
        n = int(np.prod(shp[1:]))
        assert n <= 2048, shp
        dst = slot.ap[:, 0:n].rearrange("p (a b) -> p a b", a=shp[1])
        v = R(slot, dst)
        f.dma(pool, v, dram_ap)
        issued[i] = v

    def wload(name, *idx):
        i = ring_pos[0]
        ring_pos[0] += 1
        if wl_seq is None:
            wl_rec.append((name, idx))
            _issue(i, name, idx)
        else:
            assert wl_seq[i] == (name, idx), (i, wl_seq[i], name, idx)
            for j in range(i, min(i + LOOKAHEAD + 1, len(wl_seq))):
                if j not in issued:
                    _issue(j, *wl_seq[j])
        return issued.pop(i)

    kcp = lambda d: d.rearrange("(kc p) f -> p kc f", p=128)
    for nm, d in (("f1g", f1g_d), ("f1u", f1u_d), ("f1d", f1d_d), ("f2g", f2g_d), ("f2u", f2u_d), ("f2d", f2d_d),
                  ("win", win_d), ("wkrot", wkrot_d), ("wuq", wuq_d), ("wuqr", wuqr_d), ("wukv", wukv_d),
                  ("wo_mla", wo_mla_d), ("wo_gla", wo_gla_d), ("wo_x", wo_x_d), ("xwkv", xwkv_d), ("wout", wout_d)):
        wviews[nm] = kcp(d)
    A_ = slice(None)

    def cs(a, n):
        return slice(a, a + n)

    f.dma(sp, vecs, vecs_d)
    f.dma(sp, utri, utri_d)
    f.dma(sp, sltri, sltri_d)
    f.dma(sp, wa2_t, wa2_d)
    for half in range(2):
        for c in range(8):
            dst = R(hT[c][2 * half], hT_h[:, c, half * 1024:(half + 1) * 1024])
            ev = f.dma(sp, dst, xT_d[c * 128:(c + 1) * 128, half * 1024:(half + 1) * 1024])
            hT[c][2 * half + 1].w = ev
    memset(ones_bf, 1.0)
    memset(ones_f, 1.0)
    memset(eps_t, EPS)
    memset(gaug, 1.0)
    memset(gstate, 0.0)
    for i in range(4):
        memset(gstate_bf[i], 0.0)
        memset(ktz_all[i], 0.0)
    for kc in range(16):
        memset(vaug[kc][:, :, 64:65], 1.0, eng=pool)
    memset(vaug_pad, 0.0, eng=pool)
    for h4 in range(4):
        cp(utri4[:, h4, :], utri, eng=dve)
    f.dma(pool, ident_bf, ident_d)
    ts(negm_bf, utri, -1.0, 30000.0, ALU.add, ALU.mult)

    def vcol(c):
        return vecs[:, c:c + 1]

    def rmsnorm(srcs, gcols, dim, outs, n, extra=None):
        p = ps_get()
        for i, s in enumerate(srcs):
            sq = scr.get()
            actf(sq.bf()[:, 0:n], s, AF.Square)
            mm(p[:, 0:n], ones_bf, sq.bf()[:, 0:n], i == 0, i == len(srcs) - 1)
            scr.put(sq)
        rstd = scr.get()
        actf(rstd[:, 0:n], p[:, 0:n], AF.Ln, scale=1.0 / dim, bias=eps_t)
        actf(rstd[:, 0:n], rstd[:, 0:n], AF.Exp, scale=-0.5)
        for i, s in enumerate(srcs):
            stt(outs[i], s, gcols[i], rstd[:, 0:n], ALU.mult, ALU.mult)
        scr.put(rstd)

    def ffn(ng, nu, nd, vbase, on_tile_done=None, end_barrier=True):
        nT = [[T(B_h[:, c * S + t * TT: c * S + (t + 1) * TT], f"nT{c}_{t}") for t in range(NT)] for c in range(8)]
        for t in range(NT):
            rmsnorm([hT[c][t] for c in range(8)], [vcol(vbase + c) for c in range(8)], D,
                    [nT[c][t] for c in range(8)], TT)
        NG = DFF // 256

        def loads(g):
            return (wload(ng, A_, cs(g * 256, 256)), wload(nu, A_, cs(g * 256, 256)), wload(nd, cs(2 * g, 2), A_))

        pending = [None]

        def down(wd, a_tiles, t, last=False):
            for c in range(8):
                py = ps_get()
                for jj in range(2):
                    mm(py, wd[:, jj, c * 128:(c + 1) * 128], a_tiles[jj].bf()[:, 0:TT], jj == 0, jj == 1)
                stt(hT[c][t], py, 0.5, hT[c][t], ALU.mult, ALU.add)
            scr.put(*a_tiles)
            if last and on_tile_done is not None:
                on_tile_done(t)

        for g in range(NG):
            if pending[0] is not None:
                pending[0]()
                pending[0] = None
            wg, wu, wd = loads(g)
            for t in range(NT):
                a_tiles = []
                for jj in range(2):
                    pg = ps_get()
                    pu = ps_get()
                    for k in range(8):
                        mm(pg, wg[:, k, jj * 128:(jj + 1) * 128], nT[k][t], k == 0, k == 7)
                    for k in range(8):
                        mm(pu, wu[:, k, jj * 128:(jj + 1) * 128], nT[k][t], k == 0, k == 7)
                    s = scr.get()
                    actf(s, pg, AF.Silu)
                    a = scr.get()
                    tt(a.bf()[:, 0:TT], s, pu, ALU.mult)
                    scr.put(s)
                    a_tiles.append(a)
                if pending[0] is not None:
                    pending[0]()
                pending[0] = (lambda wd=wd, a_tiles=a_tiles, t=t, g=g: down(wd, a_tiles, t, last=(g == NG - 1)))
        pending[0]()
        if end_barrier:
            f.barrier()

    def mixer():
        kT = [[T(B_h[0:96, h * S + t * TT: h * S + (t + 1) * TT], f"kT{h}_{t}") for t in range(NT)] for h in range(8)]

        def proj_fm(wslot, col0, ncols, src, n, nk=8):
            p = ps_get()
            for k in range(nk):
                mm(p[0:ncols, 0:n], wslot[:, k, col0:col0 + ncols], src[k], k == 0, k == nk - 1)
            return p

        memf = [scr.get() for _ in range(4)]
        memv = memT_d.rearrange("(kc p) m -> p kc m", p=128)
        for i in range(4):
            f.dma(pool, memf[i][:, 0:512].re("p (a b) -> p a b", a=2), memv[:, 2 * i:2 * i + 2, :])
        memn = [scr.get() for _ in range(2)]
        msrc = [memf[c // 2][:, (c % 2) * 256:(c % 2 + 1) * 256] for c in range(8)]
        mdst = [memn[c // 4].bf()[:, (c % 4) * 256:(c % 4 + 1) * 256] for c in range(8)]
        rmsnorm(msrc, [vcol(V_MEM + c) for c in range(8)], D, mdst, NMEM)
        scr.put(*memf)
        for hh in range(4):
            if hh % 2 == 0:
                wk = wload("xwkv", A_, cs(hh * 128, 256))
            p = ps_get()
            for k in range(8):
                mm(p[:, 0:NMEM], wk[:, k, (hh % 2) * 128:(hh % 2 + 1) * 128], mdst[k], k == 0, k == 7)
            cp(kxT[:, hh, :], p[:, 0:NMEM])
        for half in range(2):
            wv = wload("xwkv", A_, cs(512 + half * 256, 256))
            for mc in range(2):
                p = ps_get()
                for k in range(8):
                    mm(p[:, 0:256], mdst[k][:, mc * 128:(mc + 1) * 128], wv[:, k, :], k == 0, k == 7)
                cp(vx[:, mc, half * 256:(half + 1) * 256], p[:, 0:256])
        scr.put(*memn)

        TWO_PI = 2.0 * math.pi
        C1 = 6.28125
        C2 = TWO_PI - C1
        utri_bf = utri4[:, 0, :]

        def make_tables(t):
            tsl = slice(t * TT, (t + 1) * TT)
            Ct, St = scr.get(), scr.get()
            pos_i = scr.get()
            f.dma(sp, pos_i.i32()[64:96, :], posr_d[:, tsl])
            f.dma(sp, pos_i.i32()[96:128, :], posr_d[:, tsl])
            a2, kf, m = scr.get(), scr.get(), scr.get()
            cp(a2[64:128, :], pos_i.i32()[64:128, :], eng=dve)
            ts(a2[64:128, :], a2[64:128, :], vecs[64:128, V_IF:V_IF + 1], None, ALU.mult)
            ts(a2[96:128, :], a2[96:128, :], math.pi / 2, None, ALU.add)
            ts(kf[64:128, :], a2[64:128, :], 1.0 / TWO_PI, None, ALU.mult)
            cp(m.i32()[64:128, :], kf[64:128, :], eng=dve)
            cp(kf[64:128, :], m.i32()[64:128, :], eng=dve)
            stt(a2[64:128, :], kf[64:128, :], -C1, a2[64:128, :], ALU.mult, ALU.add)
            stt(a2[64:128, :], kf[64:128, :], -C2, a2[64:128, :], ALU.mult, ALU.add)
            ts(m[64:128, :], a2[64:128, :], math.pi, TWO_PI, ALU.is_gt, ALU.mult)
            tt(a2[64:128, :], a2[64:128, :], m[64:128, :], ALU.subtract)
            ts(m[64:128, :], a2[64:128, :], -math.pi, TWO_PI, ALU.is_lt, ALU.mult)
            tt(a2[64:128, :], a2[64:128, :], m[64:128, :], ALU.add)
            ts(a2[64:128, :], a2[64:128, :], 3.1415925, -3.1415925, ALU.min, ALU.max)
            actf(St[64:96, :], a2[64:96, :], AF.Sin)
            actf(Ct[64:96, :], a2[96:128, :], AF.Sin)
            ts(St[64:96, :], St[64:96, :], vecs[64:96, V_SG:V_SG + 1], None, ALU.mult)
            scr.put(pos_i, a2, kf, m)
            return Ct, St

        def make_norm(t):
            nt_tiles = [scr.get() for _ in range(4)]
            nTt = [nt_tiles[c // 2].bf()[:, (c % 2) * TT:(c % 2 + 1) * TT] for c in range(8)]
            rmsnorm([hT[c][t] for c in range(8)], [vcol(V_MIX + c) for c in range(8)], D, nTt, TT)
            return nt_tiles, nTt

        normed = make_norm(0)
        tables = make_tables(0)
        for t in range(NT):
            nt_tiles, nTt = normed
            Ct, St = tables

            def rope_combine(pA, pB, outs):
                t1, t2 = scr.get(), scr.get()
                tt(t1[64:96, :], pA[64:96, :], Ct[64:96, :], ALU.mult)
                tt(t2[64:96, :], pB[64:96, :], St[64:96, :], ALU.mult)
                for o in outs:
                    tt(o, t1[64:96, :], t2[64:96, :], ALU.add)
                scr.put(t1, t2)

            cq = [scr.get() for _ in range(3)]
            ckv = [scr.get() for _ in range(2)]
            w0 = wload("win", A_, cs(0, 256))
            cp(cq[0], proj_fm(w0, 0, 128, nTt, TT))
            cp(cq[1], proj_fm(w0, 128, 128, nTt, TT))
            w1 = wload("win", A_, cs(256, 256))
            cp(cq[2], proj_fm(w1, 0, 128, nTt, TT))
            cp(ckv[0], proj_fm(w1, 128, 128, nTt, TT))
            w2 = wload("win", A_, cs(512, 160))
            cp(ckv[1], proj_fm(w2, 0, 128, nTt, TT))
            pA = proj_fm(w2, 64, 96, nTt, TT)
            w3 = wload("wkrot", A_, A_)
            pB = proj_fm(w3, 0, 96, nTt, TT)
            cqn_t = [scr.get() for _ in range(2)]
            cqn = [cqn_t[i // 2].bf()[:, (i % 2) * TT:(i % 2 + 1) * TT] for i in range(3)]
            rmsnorm(cq, [vcol(V_QN + i) for i in range(3)], 384, cqn, TT)
            ckvn_t = scr.get()
            ckvn = [ckvn_t.bf()[:, i * TT:(i + 1) * TT] for i in range(2)]
            rmsnorm(ckv, [vcol(V_KVN + i) for i in range(2)], 256, ckvn, TT)
            rope_combine(pA, pB, [kT[h][t][64:96, :] for h in range(8)])
            scr.put(*cq)
            scr.put(*ckv)

            gq_tile, gk_tile = scr.get(), scr.get()
            gq_b = [gq_tile.bf()[:, pr * TT:(pr + 1) * TT] for pr in range(2)]
            gk_b = [gk_tile.bf()[:, pr * TT:(pr + 1) * TT] for pr in range(2)]
            wg0 = wload("win", A_, cs(C_GQ, 256))
            for pr in range(2):
                cp(gq_b[pr], proj_fm(wg0, pr * 128, 128, nTt, TT))
            wg1 = wload("win", A_, cs(C_GK, 256))
            for pr in range(2):
                cp(gk_b[pr], proj_fm(wg1, pr * 128, 128, nTt, TT), eng=dve)
            gktm_tile = scr.get()
            gktm = [gktm_tile.bf()[:, i * 256:(i + 1) * 256] for i in range(4)]
            for i in range(4):
                p = ps_get()
                for k in range(8):
                    mm(p[:, 0:256], nTt[k][:, i * 128:(i + 1) * 128], wg1[:, k, :], k == 0, k == 7)
                cp(gktm[i], p[:, 0:256], eng=(dve if i % 2 else act))
            gv_t = [scr.get() for _ in range(2)]
            wv0 = wload("win", A_, cs(C_GV, 256))
            wv1 = wload("win", A_, cs(C_GV + 256, 256))
            for i in range(4):
                p = ps_get()
                for hv, wv_ in enumerate((wv0, wv1)):
                    for k in range(8):
                        mm(p[:, hv * 256:(hv + 1) * 256], nTt[k][:, i * 128:(i + 1) * 128], wv_[:, k, :], k == 0, k == 7)
                cp(gv_t[i // 2].bf()[:, (i % 2) * 512:(i % 2 + 1) * 512], p, eng=(dve if i % 2 else act))
            wga = wload("win", A_, cs(C_GA - 112, 128))
            p = proj_fm(wga, 112, 16, nTt, TT)
            cp(gaug[0:16, :], p[0:16, :])
            la_t = [scr.get() for _ in range(2)]
            for i in range(4):
                p = ps_get()
                mm(p[:, 0:256], gaug[:, i * 128:(i + 1) * 128], wa2_t, True, True)
                la = la_t[i // 2][:, (i % 2) * 256:(i % 2 + 1) * 256]
                actf(la, p[:, 0:256], AF.Exp, scale=-1.0)
                actf(la, la, AF.Ln, bias=ones_f[:, 0:1])
                ts(la, la, -1.0 / 16.0, None, ALU.mult)
            gr_t = [scr.get() for _ in range(2)]
            for hh in range(4):
                if hh % 2 == 0:
                    wgr = wload("win", A_, cs(C_GR + hh * 128, 256))
                p = proj_fm(wgr, (hh % 2) * 128, 128, nTt, TT)
                actf(gr_t[hh // 2].bf()[:, (hh % 2) * TT:(hh % 2 + 1) * TT], p, AF.Silu)

            qtT_tile = scr.get()
            qtT_all = [qtT_tile.bf()[:, pr * TT:(pr + 1) * TT] for pr in range(2)]
            la_c = [la_t[i // 2][:, (i % 2) * 256:(i % 2 + 1) * 256] for i in range(4)]
            for pr in range(2):
                pb = ps_get()
                for i in range(4):
                    mm(pb[:, i * 128:(i + 1) * 128], la_c[i][:, pr * 128:(pr + 1) * 128], utri, True, True)
                Et = scr.get()
                actf(Et, pb, AF.Exp)
                stt(qtT_all[pr], gq_b[pr], 0.125, Et, ALU.mult, ALU.mult)
                cp(gdec[:, pr * 4:(pr + 1) * 4], Et[:, :].re("p (i t) -> p i t", t=128)[:, :, 127], eng=dve)
                Eit = scr.get()
                actf(Eit, pb, AF.Exp, scale=-1.0)
                for o2 in range(2):
                    o = o2 * 64
                    tt(ktz_all[pr * 2 + o2][o:o + 64, :], gk_b[pr][o:o + 64, :], Eit[o:o + 64, :], ALU.mult)
                scr.put(Et, Eit)
            kdec_tile = scr.get()
            kdec_all = kdec_tile.bf()
            for half in range(2):
                pq = ps_get()
                for i2 in range(2):
                    mm(pq[:, i2 * 256:(i2 + 1) * 256], sltri, la_c[half * 2 + i2], True, True)
                sft = scr.get()
                actf(sft, pq, AF.Exp)
                tt(kdec_all[:, half * 512:(half + 1) * 512], gktm_tile.bf()[:, half * 512:(half + 1) * 512], sft, ALU.mult)
                scr.put(sft)
            scr.put(gq_tile, gk_tile, gktm_tile)
            scr.put(*la_t)
            attm_tiles = [scr.get(), scr.get()]
            attm_all = [attm_tiles[i // 2].bf()[:, (i % 2) * 512:(i % 2 + 1) * 512] for i in range(4)]
            for i in range(4):
                csl = slice(i * 128, (i + 1) * 128)
                pa = ps_get()
                for h4 in range(4):
                    mm(pa[:, h4 * 128:(h4 + 1) * 128], ktz_all[h4][:, csl], qtT_all[h4 // 2][:, csl], True, True)
                tt(attm_all[i], pa, utri4[:, :, :].re("p a b -> p (a b)"), ALU.mult)

            xq_tiles = [scr.get(), scr.get()]
            xq_all = [xq_tiles[hh // 2].bf()[:, (hh % 2) * TT:(hh % 2 + 1) * TT] for hh in range(4)]
            for half in range(2):
                wxq = wload("win", A_, cs(C_XQ + half * 256, 256))
                for h2 in range(2):
                    cp(xq_all[half * 2 + h2], proj_fm(wxq, h2 * 128, 128, nTt, TT), eng=(dve if h2 else act))
            ox_tiles = [scr.get() for _ in range(2)]
            oT_x = [ox_tiles[i // 2].bf()[:, (i % 2) * TT:(i % 2 + 1) * TT] for i in range(4)]
            scx = 1.0 / math.sqrt(128.0)
            for hh in range(4):
                xpo = ps_get(hold=True)
                xpden = ps_get(hold=True)
                for mc in range(2):
                    pst = ps_get()
                    mm(pst, kxT[:, hh, mc * 128:(mc + 1) * 128], xq_all[hh], True, True)
                    PT = scr.get()
                    actf(PT.bf()[:, 0:TT], pst, AF.Exp, scale=scx)
                    mm(xpo, vx[:, mc, hh * 128:(hh + 1) * 128], PT.bf()[:, 0:TT], mc == 0, mc == 1)
                    mm(xpden, ones_bf, PT.bf()[:, 0:TT], mc == 0, mc == 1)
                    scr.put(PT)
                rd = scr.get()
                actf(rd, xpden, AF.Ln)
                actf(rd, rd, AF.Exp, scale=-1.0)
                tt(oT_x[hh], xpo, rd, ALU.mult)
                ps_rel(xpo)
                ps_rel(xpden)
                scr.put(rd)
            scr.put(*xq_tiles)

            wkv = wload("wukv", A_, A_)
            for h in range(8):
                p = ps_get()
                for k in range(2):
                    mm(p[0:64, :], wkv[:, k, h * 128:h * 128 + 64], ckvn[k], k == 0, k == 1)
                cp(kT[h][t][0:64, :], p[0:64, :], eng=(dve if h % 2 else act))
            for i in range(4):
                p = ps_get()
                for k in range(2):
                    rhs = wkv[:, k, :].re("p (h two d) -> p h two d", two=2, d=64)[:, :, 1, :]
                    mm(p[:, :].re("p (h d) -> p h d", h=8), ckvn[k][:, i * 128:(i + 1) * 128], rhs, k == 0, k == 1)
                cp(vaug[4 * t + i][:, :, 0:64], p[:, :].re("p (h d) -> p h d", h=8))
            scr.put(ckvn_t)

            qt_tiles = [scr.get() for _ in range(4)]
            qT = [qt_tiles[h // 2].bf()[:, (h % 2) * TT:(h % 2 + 1) * TT] for h in range(8)]
            for qt_ in qt_tiles:
                memset(qt_.bf()[96:128, :], 0.0, eng=pool)
            for half in range(2):
                wq = wload("wuq", A_, cs(half * 384, 384))
                wqr = wload("wuqr", A_, cs(half * 384, 384))
                for hh in range(4):
                    h = half * 4 + hh
                    pA = proj_fm(wq, hh * 96, 96, cqn, TT, nk=3)
                    pB = proj_fm(wqr, hh * 96, 96, cqn, TT, nk=3)
                    cp(qT[h][0:64, :], pA[0:64, :])
                    rope_combine(pA, pB, [qT[h][64:96, :]])
            scr.put(*cqn_t)
            scr.put(Ct, St)

            om_tiles = [scr.get() for _ in range(2)]
            oT_mla = [om_tiles[i // 2].bf()[:, (i % 2) * TT:(i % 2 + 1) * TT] for i in range(4)]
            nk = 4 * t + 4
            sc = 1.0 / math.sqrt(96.0)
            pend_norm = [None]

            def flush_norm():
                if pend_norm[0] is not None:
                    pend_norm[0]()
                    pend_norm[0] = None

            ps_held.add(6)
            ps_held.add(7)
            for h in range(8):
                po = psum[6 + (h % 2)]
                kTh, qTh = kT[h], qT[h]
                nchunk = [0]

                def chunk_done():
                    nchunk[0] += 1
                    if nchunk[0] == 3:
                        flush_norm()

                def score_into(dst, kc, c0, masked=False):
                    kt128 = R(kTh[kc // 4], B_h[:, h * S + kc * 128: h * S + (kc + 1) * 128])
                    mm(dst[:, c0:TT], kt128, qTh[:, c0:TT], True, not masked)
                    if masked:
                        mm(dst[:, c0:c0 + 128], ident_bf, negm_bf, False, True)

                def pv(kc, rhs, c0):
                    w128 = R(vaug[kc], vaug_flat[:, (kc * 8 + h) * 65:(kc * 8 + h) * 65 + 128])
                    extra = [vaug[kc + 1]] if (h >= 7 and kc + 1 < 16) else ([vaug_pad] if h >= 7 else [])
                    f.op(pe, lambda e: e.matmul(po.ap[:, c0:TT], lhsT=w128.ap, rhs=_ap(rhs), start=(kc == 0),
                                                stop=(kc == nk - 1)),
                         outs=[po], ins=[w128, rhs] + extra, inc=(kc == nk - 1))

                npair = 2 * t
                pairs = []

                def issue_pair(j):
                    pa_, pb_, ap2 = ps_pair()
                    score_into(pa_, 2 * j, 0)
                    score_into(pb_, 2 * j + 1, 0)
                    pairs.append((pa_, pb_, ap2))

                for j in range(min(1, npair)):
                    issue_pair(j)
                diag_q = []

                def issue_diag(jd):
                    kc = 4 * t + jd
                    pst = ps_get()
                    score_into(pst, kc, jd * 128, masked=True)
                    diag_q.append(pst)

                if npair < 1:
                    issue_diag(0)
                nd_issued = len(diag_q)
                for j in range(npair):
                    pa_, pb_, ap2 = pairs.pop(0)
                    if j + 1 < npair:
                        issue_pair(j + 1)
                    else:
                        issue_diag(0)
                        issue_diag(1)
                        nd_issued = 2
                    PT = scr.get()
                    pt2 = PT.bf()
                    f.op(act, lambda e: e.activation(out=pt2.ap.rearrange("p (a b) -> p a b", a=2), in_=ap2,
                                                     func=AF.Exp, scale=sc), outs=[PT], ins=[pa_, pb_])
                    pv(2 * j, pt2[:, 0:TT], 0)
                    pv(2 * j + 1, pt2[:, TT:2 * TT], 0)
                    scr.put(PT)
                    chunk_done()
                    chunk_done()
                for jd in range(4):
                    kc = 4 * t + jd
                    c0 = jd * 128
                    while nd_issued < min(4, jd + 2):
                        issue_diag(nd_issued)
                        nd_issued += 1
                    pst = diag_q.pop(0)
                    PT = scr.get()
                    actf(PT.bf()[:, c0:TT], pst[:, c0:TT], AF.Exp, scale=sc)
                    pv(kc, PT.bf()[:, c0:TT], c0)
                    scr.put(PT)
                    chunk_done()
                flush_norm()
                den = scr.get()
                cp(den[0:1, :], po[64:65, :])
                pd = ps_get(hold=True)
                mm(pd[0:64, :], ones_f[0:1, 0:64], den[0:1, :], True, True)

                def norm_b(po=po, pd=pd, den=den, h=h):
                    rd = scr.get()
                    actf(rd[0:64, :], pd[0:64, :], AF.Ln)
                    actf(rd[0:64, :], rd[0:64, :], AF.Exp, scale=-1.0)
                    tt(oT_mla[h // 2][(h % 2) * 64:(h % 2) * 64 + 64, :], po[0:64, :], rd[0:64, :], ALU.mult)
                    ps_rel(pd)
                    scr.put(den, rd)

                pend_norm[0] = norm_b
            flush_norm()
            ps_held.discard(6)
            ps_held.discard(7)
            scr.put(*qt_tiles)
            if stop_after == "m_mla":
                return

            og_tiles = [scr.get() for _ in range(4)]
            for i in range(4):
                csl = slice(i * 128, (i + 1) * 128)
                attm = attm_all[i]
                po = ps_get()
                gv = gv_t[i // 2].bf()[:, (i % 2) * 512:(i % 2 + 1) * 512]
                for h4 in range(4):
                    pr = h4 // 2
                    mm(po[:, h4 * 128:(h4 + 1) * 128], gv[:, h4 * 128:(h4 + 1) * 128],
                       attm[:, h4 * 128:(h4 + 1) * 128], True, False)
                    mm(po[:, h4 * 128:(h4 + 1) * 128], gstate_bf[h4], qtT_all[pr][:, csl], False, True)
                for pr in range(2):
                    pds = ps_get()
                    mm(pds[:, 0:256], kdec_all[:, i * 256 + pr * 128:i * 256 + (pr + 1) * 128],
                       gv[:, pr * 256:(pr + 1) * 256], True, True)
                    for o2 in range(2):
                        o = o2 * 64
                        dcol = gdec[o:o + 64, pr * 4 + i:pr * 4 + i + 1]
                        dS = pds[o:o + 64, o2 * 128:(o2 + 1) * 128]
                        stt(gstate_bf[pr * 2 + o2][o:o + 64, :], gstate[o:o + 64, pr, :], dcol, dS, ALU.mult, ALU.add)
                        stt(gstate[o:o + 64, pr, :], gstate[o:o + 64, pr, :], dcol, dS, ALU.mult, ALU.add)
                for h4 in range(4):
                    cp(og_tiles[h4][:, csl], po[:, h4 * 128:(h4 + 1) * 128], eng=(dve if h4 % 2 else act))
            scr.put(*attm_tiles)
            scr.put(qtT_tile, kdec_tile)
            scr.put(*gv_t)
            ogl_tiles = [scr.get() for _ in range(2)]
            oT_gla = [ogl_tiles[i // 2].bf()[:, (i % 2) * TT:(i % 2 + 1) * TT] for i in range(4)]
            for hh in range(4):
                tmp = scr.get()
                rmsnorm([og_tiles[hh]], [vcol(V_ON)], 128, [tmp], TT)
                tt(oT_gla[hh], tmp, gr_t[hh // 2].bf()[:, (hh % 2) * TT:(hh % 2 + 1) * TT], ALU.mult)
                scr.put(tmp)
            scr.put(*og_tiles)
            scr.put(*gr_t)
            if stop_after == "m_x":
                return

            mg_tiles = [scr.get() for _ in range(4)]
            mT = [mg_tiles[c // 2].bf()[:, (c % 2) * TT:(c % 2 + 1) * TT] for c in range(8)]
            wonames = ["wo_mla", "wo_gla", "wo_x"]
            oTs = [oT_mla, oT_gla, oT_x]
            for cg in range(4):
                accs = [scr.get(), scr.get()]
                for b in range(3):
                    wz = wload("win", A_, cs(C_GZ + b * D + cg * 256, 256))
                    wo = wload(wonames[b], A_, cs(cg * 256, 256))
                    for c2 in range(2):
                        c = cg * 2 + c2
                        acc = accs[c2]
                        pz = proj_fm(wz, c2 * 128, 128, nTt, TT)
                        g = scr.get()
                        actf(g, pz, AF.Sigmoid, bias=vcol(V_GB + b * 8 + c))
                        pp = ps_get()
                        for k in range(4):
                            mm(pp, wo[:, k, c2 * 128:(c2 + 1) * 128], oTs[b][k], k == 0, k == 3)
                        if b == 0:
                            tt(acc, pp, g, ALU.mult)
                        else:
                            tt(g, pp, g, ALU.mult)
                            tt(mT[c] if b == 2 else acc, acc, g, ALU.add)
                        scr.put(g)
                scr.put(*accs)
            scr.put(*om_tiles)
            scr.put(*ogl_tiles)
            scr.put(*ox_tiles)
            scr.put(*nt_tiles)
            for cg in range(4):
                wo_ = wload("wout", A_, cs(cg * 256, 256))
                for c2 in range(2):
                    c = cg * 2 + c2
                    py = ps_get()
                    for k in range(8):
                        mm(py, wo_[:, k, c2 * 128:(c2 + 1) * 128], mT[k], k == 0, k == 7)
                    tt(hT[c][t], py, hT[c][t], ALU.add)
            scr.put(*mg_tiles)
            if t + 1 < NT:
                normed = make_norm(t + 1)
                tables = make_tables(t + 1)
        f.barrier()

    final = stop_after is None

    def emit_out(t):
        outs = [scr.get() for _ in range(8)]
        if final:
            rmsnorm([hT[c][t] for c in range(8)], [vcol(V_FIN + c) for c in range(8)], D, outs, TT)
        for c in range(8):
            src = outs[c] if final else hT[c][t]
            f.dma(sp, outT_d[c * 128:(c + 1) * 128, t * TT:(t + 1) * TT], src, is_output=True)
        scr.put(*outs)

    ffn("f1g", "f1u", "f1d", V_F1)
    if stop_after != "ffn1":
        mixer()
        if stop_after is None:
            ffn("f2g", "f2u", "f2d", V_F2, on_tile_done=emit_out, end_barrier=False)
    if not final:
        for t in range(NT):
            emit_out(t)
    f.finish()
    f.close()
    return nc, {e.name: (e.ninstr, e.nwait) for e in (pe, act, dve, pool, sp)}, wl_rec


def _prep_shared(inp):
    f32 = np.float32
    g = lambda k: np.ascontiguousarray(np.asarray(inp[k], dtype=f32))
    w_in = g("w_in")[0]
    vecs = np.zeros((128, NV), f32)

    def put(col, v):
        v = np.asarray(v, f32).reshape(-1)
        n = v.size // 128
        vecs[:, col:col + n] = v.reshape(n, 128).T

    put(V_F1, g("ffn1_norm")[0])
    put(V_MIX, g("mix_norm")[0])
    put(V_F2, g("ffn2_norm")[0])
    put(V_FIN, g("final_norm"))
    put(V_MEM, g("mem_norm")[0])
    gb = g("gate_bias")[0]
    for b in range(3):
        put(V_GB + 8 * b, gb[b])
    put(V_QN, g("mla_q_norm")[0])
    put(V_KVN, g("mla_kv_norm")[0])
    put(V_ON, g("gla_o_norm")[0])
    half = 16
    inv_freq = (10000.0 ** (-np.arange(half, dtype=np.float32) / half)).astype(f32)
    vecs[64:96, V_IF] = np.concatenate([inv_freq, inv_freq])
    vecs[96:128, V_IF] = np.concatenate([inv_freq, inv_freq])
    vecs[64:96, V_SG] = np.concatenate([-np.ones(16, f32), np.ones(16, f32)])
    jj = np.arange(128)[:, None]
    tq = np.arange(128)[None, :]
    utri = (jj <= tq).astype(f32)
    sltri = (jj > tq).astype(f32)
    ident = (jj == tq).astype(f32)
    wuq = g("mla_w_uq")[0]
    perm = []
    for h in range(8):
        b = h * 96
        perm += list(range(b, b + 64)) + list(range(b + 80, b + 96)) + list(range(b + 64, b + 80))
    wuqr = np.ascontiguousarray(wuq[:, perm])
    wkrot = np.ascontiguousarray(np.concatenate([w_in[:, 576:640], w_in[:, 656:672], w_in[:, 640:656]], axis=1))
    wa2 = np.zeros((32, 256), f32)
    wa2[0:16] = g("gla_w_a2")[0]
    wa2[16] = g("gla_b_a")[0]
    return {
        "vecs": vecs, "utri": utri, "sltri": sltri, "ident": ident,
        "f1g": g("ffn1_wg")[0], "f1u": g("ffn1_wu")[0], "f1d": g("ffn1_wd")[0],
        "f2g": g("ffn2_wg")[0], "f2u": g("ffn2_wu")[0], "f2d": g("ffn2_wd")[0],
        "win": w_in, "wkrot": wkrot, "wuq": wuq, "wuqr": wuqr, "wukv": g("mla_w_ukv")[0], "wa2": wa2,
        "wo_mla": g("mla_w_o")[0], "wo_gla": g("gla_w_o")[0], "wo_x": g("x_w_o")[0],
        "xwkv": g("x_w_kv")[0], "wout": g("w_out")[0],
    }


_CACHE = {}


def kernel(stop_after=None, n_cores=8, **inp):
    key = stop_after
    if key not in _CACHE:
        _, _, rec = build_program(stop_after)
        _CACHE[key] = build_program(stop_after, wl_seq=rec)
    nc, stats, _ = _CACHE[key]
    shared = _prep_shared(inp)
    x = np.asarray(inp["x"], np.float32)
    mem = np.asarray(inp["mem"], np.float32)
    pos = np.asarray(inp["positions"], np.int32)
    in_maps = []
    for b in range(n_cores):
        m = dict(shared)
        m["xT"] = np.ascontiguousarray(x[b].T)
        m["memT"] = np.ascontiguousarray(mem[b].T)
        m["posr"] = np.ascontiguousarray(np.broadcast_to(pos[b][None, :], (32, S)))
        in_maps.append(m)
    res = run_bass_kernel_spmd(nc, in_maps, core_ids=list(range(n_cores)))
    out = np.stack([np.ascontiguousarray(r["outT"].T) for r in res.results], axis=0)
    return out.astype(np.float32)
```

```python
import math
import numpy as np
import concourse.bass as bass
import concourse.mybir as mybir
from concourse.bass_utils import run_bass_kernel_spmd

F32 = mybir.dt.float32
BF16 = mybir.dt.bfloat16
I32 = mybir.dt.int32
AF = mybir.ActivationFunctionType
ALU = mybir.AluOpType

D = 1024
S = 2048
NMEM = 256
DFF = 2816
DIN = 5808
EPS = 1e-6
TT = 512
NT = S // TT


class T:
    def __init__(self, ap, name=""):
        self.ap = ap
        self.name = name
        self.w = None
        self.r = []

    def __getitem__(self, idx):
        return R(self, self.ap[idx])

    def bf(self):
        return R(self, self.ap.bitcast(BF16))

    def i32(self):
        return R(self, self.ap.bitcast(I32))


class R:
    def __init__(self, t, ap):
        self.t = t
        self.ap = ap

    def __getitem__(self, idx):
        return R(self.t, self.ap[idx])

    def re(self, s, **kw):
        return R(self.t, self.ap.rearrange(s, **kw))


def _ap(x):
    return x.ap if isinstance(x, (T, R)) else x


def _t(x):
    if isinstance(x, T):
        return x
    if isinstance(x, R):
        return x.t
    return None


class Eng:
    def __init__(self, name, h, semidx, sem):
        self.name = name
        self.h = h
        self.semidx = semidx
        self.sem = sem
        self.count = 0
        self.known = {}
        self.ninstr = 0
        self.nwait = 0


class FW:
    def __init__(self, nc, n_dma_sems=48):
        self.nc = nc
        self.sems = []
        self._stack = []
        self.pe = self._mk("pe", nc.tensor)
        self.act = self._mk("act", nc.scalar)
        self.dve = self._mk("dve", nc.vector)
        self.pool = self._mk("pool", nc.gpsimd)
        self.sp = self._mk("sp", nc.sync)
        self.dma_sems = []
        for i in range(n_dma_sems):
            idx = self._new_sem(f"dma{i}")
            self.dma_sems.append([idx, 0])
        self.dma_rr = 0
        self.out_events = []

    def _new_sem(self, name):
        cm = self.nc.semaphore(name)
        s = cm.__enter__()
        self._stack.append(cm)
        self.sems.append(s)
        return len(self.sems) - 1

    def _mk(self, name, h):
        idx = self._new_sem("s_" + name)
        return Eng(name, h, idx, self.sems[idx])

    def close(self):
        for cm in reversed(self._stack):
            cm.__exit__(None, None, None)
        self._stack = []

    def sb(self, name, shape, dtype):
        cm = self.nc.sbuf_tensor(name, list(shape), dtype)
        h = cm.__enter__()
        self._stack.append(cm)
        return h

    def ps(self, name, shape, dtype=F32):
        cm = self.nc.psum_tensor(name, list(shape), dtype)
        h = cm.__enter__()
        self._stack.append(cm)
        return h

    def _wait(self, eng, events):
        need = {}
        for ev in events:
            if ev is None:
                continue
            s, v, _ = ev
            if v > need.get(s, 0):
                need[s] = v
        for s, v in need.items():
            if eng.known.get(s, 0) >= v:
                continue
            eng.h.wait_ge(self.sems[s], v)
            eng.known[s] = v
            eng.nwait += 1

    def _deps(self, eng, outs, ins):
        evs = []
        for x in ins:
            t = _t(x)
            if t is None or t.w is None:
                continue
            if t.w[2] == eng.name and eng.name == "pe":
                continue
            evs.append(t.w)
        for x in outs:
            t = _t(x)
            if t is None:
                continue
            if t.w is not None and t.w[2] != eng.name:
                evs.append(t.w)
            for ev in t.r:
                if ev[2] != eng.name:
                    evs.append(ev)
        return evs

    def op(self, eng, fn, outs=(), ins=(), inc=True):
        self._wait(eng, self._deps(eng, outs, ins))
        instr = fn(eng.h)
        eng.ninstr += 1
        if inc:
            instr.then_inc(eng.sem, 1)
            eng.count += 1
            ev = (eng.semidx, eng.count, eng.name)
        else:
            ev = (eng.semidx, eng.count + 1, eng.name)
        for x in ins:
            t = _t(x)
            if t is not None:
                t.r.append(ev)
                if len(t.r) > 64:
                    t.r = t.r[-48:]
        for x in outs:
            t = _t(x)
            if t is not None:
                t.w = ev
                t.r = []
        return instr

    def dma(self, q, out, in_, is_output=False, **kw):
        outs = [out] if _t(out) is not None else []
        ins = [in_] if _t(in_) is not None else []
        evs = []
        for x in ins:
            t = _t(x)
            if t.w is not None:
                evs.append(t.w)
        for x in outs:
            t = _t(x)
            if t.w is not None:
                evs.append(t.w)
            evs.extend(t.r)
        slot = self.dma_sems[self.dma_rr % len(self.dma_sems)]
        self.dma_rr += 1
        if slot[1] > 0:
            evs.append((slot[0], slot[1], "dma"))
        self._wait(q, evs)
        instr = q.h.dma_start(out=_ap(out), in_=_ap(in_), **kw)
        q.ninstr += 1
        slot[1] += 16
        instr.then_inc(self.sems[slot[0]], 16)
        ev = (slot[0], slot[1], "dma")
        for x in ins:
            _t(x).r.append(ev)
        for x in outs:
            t = _t(x)
            t.w = ev
            t.r = []
        if is_output:
            self.out_events.append(ev)
        return ev

    def barrier(self):
        engs = (self.pe, self.act, self.dve)
        for e in engs:
            evs = [(o.semidx, o.count, o.name) for o in engs if o is not e and o.count > 0]
            self._wait(e, evs)

    def finish(self):
        evs = list(self.out_events)
        for slot in self.dma_sems:
            if slot[1] > 0:
                evs.append((slot[0], slot[1], "dma"))
        self._wait(self.sp, evs)


class Pool_:
    def __init__(self, tiles):
        self.free = list(tiles)
        self.n = len(tiles)

    def get(self):
        assert self.free, "scratch pool exhausted"
        self.low = min(getattr(self, "low", 999), len(self.free) - 1)
        return self.free.pop(0)

    def put(self, *ts):
        for t in ts:
            assert t not in self.free
            self.free.append(t)


C_CQ, C_CKV, C_KR, C_GQ, C_GK, C_GV, C_GA, C_GR, C_XQ, C_GZ = 0, 384, 640, 672, 928, 1184, 1696, 1712, 2224, 2736
V_F1, V_MIX, V_F2, V_FIN, V_MEM, V_GB, V_QN, V_KVN, V_ON, V_IF, V_SG = 0, 8, 16, 24, 32, 40, 64, 67, 69, 70, 71
NV = 72


LOOKAHEAD = 3


def build_program(stop_after=None, wl_seq=None):
    nc = bass.Bass("TRN2", target_bir_lowering=False)
    wl_rec = []

    def din(name, shape, dt=F32):
        return nc.dram_tensor(name, list(shape), dt, kind="ExternalInput").ap()

    xT_d = din("xT", [D, S])
    memT_d = din("memT", [D, NMEM])
    posr_d = din("posr", [32, S], I32)
    vecs_d = din("vecs", [128, NV])
    utri_d = din("utri", [128, 128])
    sltri_d = din("sltri", [128, 128])
    ident_d = din("ident", [128, 128])
    f1g_d, f1u_d, f1d_d = din("f1g", [D, DFF]), din("f1u", [D, DFF]), din("f1d", [DFF, D])
    f2g_d, f2u_d, f2d_d = din("f2g", [D, DFF]), din("f2u", [D, DFF]), din("f2d", [DFF, D])
    win_d = din("win", [D, DIN])
    wkrot_d = din("wkrot", [D, 96])
    wuq_d = din("wuq", [384, 768])
    wuqr_d = din("wuqr", [384, 768])
    wukv_d = din("wukv", [256, 1024])
    wa2_d = din("wa2", [32, 256])
    wo_mla_d, wo_gla_d, wo_x_d = din("wo_mla", [512, D]), din("wo_gla", [512, D]), din("wo_x", [512, D])
    xwkv_d = din("xwkv", [D, D])
    wout_d = din("wout", [D, D])
    outT_d = nc.dram_tensor("outT", [D, S], F32, kind="ExternalOutput").ap()

    f = FW(nc)
    pe, act, dve, pool, sp = f.pe, f.act, f.dve, f.pool, f.sp

    hT_h = f.sb("hT", [128, 8, S], F32)
    hT = [[T(hT_h[:, c, t * TT:(t + 1) * TT], f"hT{c}_{t}") for t in range(NT)] for c in range(8)]
    B_h = f.sb("B", [128, 8 * S], BF16)
    vaug_flat = f.sb("vaug", [128, 16 * 8 * 65 + 64], BF16)
    vaug_h = vaug_flat[:, 0:16 * 8 * 65].rearrange("p (k h d) -> p k h d", k=16, h=8)
    vaug = [T(vaug_h[:, kc], f"vaug{kc}") for kc in range(16)]
    vaug_pad = T(vaug_flat[:, 16 * 8 * 65:], "vaug_pad")
    NSLOT = 6
    ring = [T(f.sb(f"ring{i}", [128, 2048], BF16)[:], f"ring{i}") for i in range(NSLOT)]
    ring_pos = [0]
    NSCR = 27
    scr = Pool_([T(f.sb(f"scr{i}", [128, 512], F32)[:], f"scr{i}") for i in range(NSCR)])
    ps_all = f.ps("ps_all", [128, 8, 512], F32)
    psum = [T(ps_all[:, i, :], f"ps{i}") for i in range(8)]
    pair_pos = [0]
    ps_pos = [0]

    ps_held = set()
    ps_last = [0] * 8
    ps_clock = [0]

    def _stamp(i):
        ps_clock[0] += 1
        ps_last[i] = ps_clock[0]

    def ps_get(hold=False):
        cand = [i for i in range(8) if i not in ps_held]
        i = min(cand, key=lambda b: ps_last[b])
        _stamp(i)
        if hold:
            ps_held.add(i)
        return psum[i]

    def ps_rel(p):
        ps_held.discard(psum.index(p))

    def ps_take(i):
        ps_held.add(i)
        _stamp(i)
        return psum[i]

    def ps_pair():
        cand = [j for j in range(4) if 2 * j not in ps_held and 2 * j + 1 not in ps_held]
        j = min(cand, key=lambda q: max(ps_last[2 * q], ps_last[2 * q + 1]))
        _stamp(2 * j)
        _stamp(2 * j + 1)
        return psum[2 * j], psum[2 * j + 1], ps_all[:, 2 * j:2 * j + 2, :]

    vecs = T(f.sb("vecs_s", [128, NV], F32)[:], "vecs")
    utri = T(f.sb("utri_s", [128, 128], F32)[:], "utri")
    sltri = T(f.sb("sltri_s", [128, 128], F32)[:], "sltri")
    utri4 = T(f.sb("utri4", [128, 4, 128], BF16)[:], "utri4")
    ident_bf = T(f.sb("ident_bf", [128, 128], BF16)[:], "ident_bf")
    negm_bf = T(f.sb("negm_bf", [128, 128], BF16)[:], "negm_bf")
    ones_bf = T(f.sb("ones_bf", [128, 128], BF16)[:], "ones_bf")
    ones_f = T(f.sb("ones_f", [128, 128], F32)[:], "ones_f")
    eps_t = T(f.sb("eps", [128, 1], F32)[:], "eps")
    wa2_t = T(f.sb("wa2_s", [32, 256], F32)[:], "wa2")
    gaug = T(f.sb("gaug", [32, TT], F32)[:], "gaug")
    gstate = T(f.sb("gstate", [128, 2, 128], F32)[:], "gstate")
    gstate_bf = [T(f.sb(f"gstate_bf{i}", [128, 128], BF16)[:], f"gstate_bf{i}") for i in range(4)]
    ktz_all = [T(f.sb(f"ktz{i}", [128, TT], BF16)[:], f"ktz{i}") for i in range(4)]
    gdec = T(f.sb("gdec", [128, 8], F32)[:], "gdec")
    kxT = T(f.sb("kxT", [128, 4, NMEM], BF16)[:], "kxT")
    vx = T(f.sb("vx", [128, 2, 512], BF16)[:], "vx")

    def mm(out, lhsT, rhs, start, stop):
        f.op(pe, lambda e: e.matmul(_ap(out), lhsT=_ap(lhsT), rhs=_ap(rhs), start=start, stop=stop),
             outs=[out], ins=[lhsT, rhs], inc=stop)

    def actf(out, in_, func, scale=1.0, bias=None):
        ins = [in_]
        kw = {}
        if bias is not None:
            kw["bias"] = _ap(bias)
            ins.append(bias)
        if isinstance(scale, (T, R)):
            ins.append(scale)
        f.op(act, lambda e: e.activation(out=_ap(out), in_=_ap(in_), func=func, scale=_ap(scale), **kw),
             outs=[out], ins=ins)

    def tt(out, a, b, op, eng=None):
        f.op(eng or dve, lambda e: e.tensor_tensor(out=_ap(out), in0=_ap(a), in1=_ap(b), op=op),
             outs=[out], ins=[a, b])

    def stt(out, in0, scalar, in1, op0, op1):
        ins = [in0, in1] + ([scalar] if isinstance(scalar, (T, R)) else [])
        f.op(dve, lambda e: e.scalar_tensor_tensor(out=_ap(out), in0=_ap(in0), scalar=_ap(scalar), in1=_ap(in1),
                                                   op0=op0, op1=op1), outs=[out], ins=ins)

    def ts(out, in0, s1, s2, op0, op1=None, eng=None):
        ins = [in0] + [s for s in (s1, s2) if isinstance(s, (T, R))]
        if op1 is None:
            f.op(eng or dve, lambda e: e.tensor_scalar(out=_ap(out), in0=_ap(in0), scalar1=_ap(s1), scalar2=None,
                                                       op0=op0), outs=[out], ins=ins)
        else:
            f.op(eng or dve, lambda e: e.tensor_scalar(out=_ap(out), in0=_ap(in0), scalar1=_ap(s1), scalar2=_ap(s2),
                                                       op0=op0, op1=op1), outs=[out], ins=ins)

    def cp(out, in_, eng=None):
        eng = eng or act
        if eng is act:
            f.op(act, lambda e: e.activation(out=_ap(out), in_=_ap(in_), func=AF.Copy), outs=[out], ins=[in_])
        else:
            f.op(eng, lambda e: e.tensor_copy(out=_ap(out), in_=_ap(in_)), outs=[out], ins=[in_])

    def memset(x, val, eng=None):
        f.op(eng or dve, lambda e: e.memset(_ap(x), val), outs=[x])

    wviews = {}
    issued = {}

    def _issue(i, name, idx):
        dram_ap = wviews[name][(slice(None),) + tuple(idx)]
        slot = ring[i % NSLOT]
        shp = dram_ap.shape
        n = int(np.prod(shp[1:]))
        assert n <= 2048, shp
        dst = slot.ap[:, 0:n].rearrange("p (a b) -> p a b", a=shp[1])
        v = R(slot, dst)
        f.dma(pool, v, dram_ap)
        issued[i] = v

    def wload(name, *idx):
        i = ring_pos[0]
        ring_pos[0] += 1
        if wl_seq is None:
            wl_rec.append((name, idx))
            _issue(i, name, idx)
        else:
            assert wl_seq[i] == (name, idx), (i, wl_seq[i], name, idx)
            for j in range(i, min(i + LOOKAHEAD + 1, len(wl_seq))):
                if j not in issued:
                    _issue(j, *wl_seq[j])
        return issued.pop(i)

    kcp = lambda d: d.rearrange("(kc p) f -> p kc f", p=128)
    for nm, d in (("f1g", f1g_d), ("f1u", f1u_d), ("f1d", f1d_d), ("f2g", f2g_d), ("f2u", f2u_d), ("f2d", f2d_d),
                  ("win", win_d), ("wkrot", wkrot_d), ("wuq", wuq_d), ("wuqr", wuqr_d), ("wukv", wukv_d),
                  ("wo_mla", wo_mla_d), ("wo_gla", wo_gla_d), ("wo_x", wo_x_d), ("xwkv", xwkv_d), ("wout", wout_d)):
        wviews[nm] = kcp(d)
    A_ = slice(None)

    def cs(a, n):
        return slice(a, a + n)

    f.dma(sp, vecs, vecs_d)
    f.dma(sp, utri, utri_d)
    f.dma(sp, sltri, sltri_d)
    f.dma(sp, wa2_t, wa2_d)
    for half in range(2):
        for c in range(8):
            dst = R(hT[c][2 * half], hT_h[:, c, half * 1024:(half + 1) * 1024])
            ev = f.dma(sp, dst, xT_d[c * 128:(c + 1) * 128, half * 1024:(half + 1) * 1024])
            hT[c][2 * half + 1].w = ev
    memset(ones_bf, 1.0)
    memset(ones_f, 1.0)
    memset(eps_t, EPS)
    memset(gaug, 1.0)
    memset(gstate, 0.0)
    for i in range(4):
        memset(gstate_bf[i], 0.0)
        memset(ktz_all[i], 0.0)
    for kc in range(16):
        memset(vaug[kc][:, :, 64:65], 1.0, eng=pool)
    memset(vaug_pad, 0.0, eng=pool)
    for h4 in range(4):
        cp(utri4[:, h4, :], utri, eng=dve)
    f.dma(pool, ident_bf, ident_d)
    ts(negm_bf, utri, -1.0, 30000.0, ALU.add, ALU.mult)

    def vcol(c):
        return vecs[:, c:c + 1]

    def rmsnorm(srcs, gcols, dim, outs, n, extra=None):
        p = ps_get()
        for i, s in enumerate(srcs):
            sq = scr.get()
            actf(sq.bf()[:, 0:n], s, AF.Square)
            mm(p[:, 0:n], ones_bf, sq.bf()[:, 0:n], i == 0, i == len(srcs) - 1)
            scr.put(sq)
        rstd = scr.get()
        actf(rstd[:, 0:n], p[:, 0:n], AF.Ln, scale=1.0 / dim, bias=eps_t)
        actf(rstd[:, 0:n], rstd[:, 0:n], AF.Exp, scale=-0.5)
        for i, s in enumerate(srcs):
            stt(outs[i], s, gcols[i], rstd[:, 0:n], ALU.mult, ALU.mult)
        scr.put(rstd)

    def ffn(ng, nu, nd, vbase, on_tile_done=None, end_barrier=True):
        nT = [[T(B_h[:, c * S + t * TT: c * S + (t + 1) * TT], f"nT{c}_{t}") for t in range(NT)] for c in range(8)]
        for t in range(NT):
            rmsnorm([hT[c][t] for c in range(8)], [vcol(vbase + c) for c in range(8)], D,
                    [nT[c][t] for c in range(8)], TT)
        NG = DFF // 256

        def loads(g):
            return (wload(ng, A_, cs(g * 256, 256)), wload(nu, A_, cs(g * 256, 256)), wload(nd, cs(2 * g, 2), A_))

        pending = [None]

        def down(wd, a_tiles, t, last=False):
            for c in range(8):
                py = ps_get()
                for jj in range(2):
                    mm(py, wd[:, jj, c * 128:(c + 1) * 128], a_tiles[jj].bf()[:, 0:TT], jj == 0, jj == 1)
                stt(hT[c][t], py, 0.5, hT[c][t], ALU.mult, ALU.add)
            scr.put(*a_tiles)
            if last and on_tile_done is not None:
                on_tile_done(t)

        for g in range(NG):
            if pending[0] is not None:
                pending[0]()
                pending[0] = None
            wg, wu, wd = loads(g)
            for t in range(NT):
                a_tiles = []
                for jj in range(2):
                    pg = ps_get()
                    pu = ps_get()
                    for k in range(8):
                        mm(pg, wg[:, k, jj * 128:(jj + 1) * 128], nT[k][t], k == 0, k == 7)
                    for k in range(8):
                        mm(pu, wu[:, k, jj * 128:(jj + 1) * 128], nT[k][t], k == 0, k == 7)
                    s = scr.get()
                    actf(s, pg, AF.Silu)
                    a = scr.get()
                    tt(a.bf()[:, 0:TT], s, pu, ALU.mult)
                    scr.put(s)
                    a_tiles.append(a)
                if pending[0] is not None:
                    pending[0]()
                pending[0] = (lambda wd=wd, a_tiles=a_tiles, t=t, g=g: down(wd, a_tiles, t, last=(g == NG - 1)))
        pending[0]()
        if end_barrier:
            f.barrier()

    def mixer():
        kT = [[T(B_h[0:96, h * S + t * TT: h * S + (t + 1) * TT], f"kT{h}_{t}") for t in range(NT)] for h in range(8)]

        def proj_fm(wslot, col0, ncols, src, n, nk=8):
            p = ps_get()
            for k in range(nk):
                mm(p[0:ncols, 0:n], wslot[:, k, col0:col0 + ncols], src[k], k == 0, k == nk - 1)
            return p

        memf = [scr.get() for _ in range(4)]
        memv = memT_d.rearrange("(kc p) m -> p kc m", p=128)
        for i in range(4):
            f.dma(pool, memf[i][:, 0:512].re("p (a b) -> p a b", a=2), memv[:, 2 * i:2 * i + 2, :])
        memn = [scr.get() for _ in range(2)]
        msrc = [memf[c // 2][:, (c % 2) * 256:(c % 2 + 1) * 256] for c in range(8)]
        mdst = [memn[c // 4].bf()[:, (c % 4) * 256:(c % 4 + 1) * 256] for c in range(8)]
        rmsnorm(msrc, [vcol(V_MEM + c) for c in range(8)], D, mdst, NMEM)
        scr.put(*memf)
        for hh in range(4):
            if hh % 2 == 0:
                wk = wload("xwkv", A_, cs(hh * 128, 256))
            p = ps_get()
            for k in range(8):
                mm(p[:, 0:NMEM], wk[:, k, (hh % 2) * 128:(hh % 2 + 1) * 128], mdst[k], k == 0, k == 7)
            cp(kxT[:, hh, :], p[:, 0:NMEM])
        for half in range(2):
            wv = wload("xwkv", A_, cs(512 + half * 256, 256))
            for mc in range(2):
                p = ps_get()
                for k in range(8):
                    mm(p[:, 0:256], mdst[k][:, mc * 128:(mc + 1) * 128], wv[:, k, :], k == 0, k == 7)
                cp(vx[:, mc, half * 256:(half + 1) * 256], p[:, 0:256])
        scr.put(*memn)

        TWO_PI = 2.0 * math.pi
        C1 = 6.28125
        C2 = TWO_PI - C1
        utri_bf = utri4[:, 0, :]

        def make_tables(t):
            tsl = slice(t * TT, (t + 1) * TT)
            Ct, St = scr.get(), scr.get()
            pos_i = scr.get()
            f.dma(sp, pos_i.i32()[64:96, :], posr_d[:, tsl])
            f.dma(sp, pos_i.i32()[96:128, :], posr_d[:, tsl])
            a2, kf, m = scr.get(), scr.get(), scr.get()
            cp(a2[64:128, :], pos_i.i32()[64:128, :], eng=dve)
            ts(a2[64:128, :], a2[64:128, :], vecs[64:128, V_IF:V_IF + 1], None, ALU.mult)
            ts(a2[96:128, :], a2[96:128, :], math.pi / 2, None, ALU.add)
            ts(kf[64:128, :], a2[64:128, :], 1.0 / TWO_PI, None, ALU.mult)
            cp(m.i32()[64:128, :], kf[64:128, :], eng=dve)
            cp(kf[64:128, :], m.i32()[64:128, :], eng=dve)
            stt(a2[64:128, :], kf[64:128, :], -C1, a2[64:128, :], ALU.mult, ALU.add)
            stt(a2[64:128, :], kf[64:128, :], -C2, a2[64:128, :], ALU.mult, ALU.add)
            ts(m[64:128, :], a2[64:128, :], math.pi, TWO_PI, ALU.is_gt, ALU.mult)
            tt(a2[64:128, :], a2[64:128, :], m[64:128, :], ALU.subtract)
            ts(m[64:128, :], a2[64:128, :], -math.pi, TWO_PI, ALU.is_lt, ALU.mult)
            tt(a2[64:128, :], a2[64:128, :], m[64:128, :], ALU.add)
            ts(a2[64:128, :], a2[64:128, :], 3.1415925, -3.1415925, ALU.min, ALU.max)
            actf(St[64:96, :], a2[64:96, :], AF.Sin)
            actf(Ct[64:96, :], a2[96:128, :], AF.Sin)
            ts(St[64:96, :], St[64:96, :], vecs[64:96, V_SG:V_SG + 1], None, ALU.mult)
            scr.put(pos_i, a2, kf, m)
            return Ct, St

        def make_norm(t):
            nt_tiles = [scr.get() for _ in range(4)]
            nTt = [nt_tiles[c // 2].bf()[:, (c % 2) * TT:(c % 2 + 1) * TT] for c in range(8)]
            rmsnorm([hT[c][t] for c in range(8)], [vcol(V_MIX + c) for c in range(8)], D, nTt, TT)
            return nt_tiles, nTt

        normed = make_norm(0)
        tables = make_tables(0)
        for t in range(NT):
            nt_tiles, nTt = normed
            Ct, St = tables

            def rope_combine(pA, pB, outs):
                t1, t2 = scr.get(), scr.get()
                tt(t1[64:96, :], pA[64:96, :], Ct[64:96, :], ALU.mult)
                tt(t2[64:96, :], pB[64:96, :], St[64:96, :], ALU.mult)
                for o in outs:
                    tt(o, t1[64:96, :], t2[64:96, :], ALU.add)
                scr.put(t1, t2)

            cq = [scr.get() for _ in range(3)]
            ckv = [scr.get() for _ in range(2)]
            w0 = wload("win", A_, cs(0, 256))
            cp(cq[0], proj_fm(w0, 0, 128, nTt, TT))
            cp(cq[1], proj_fm(w0, 128, 128, nTt, TT))
            w1 = wload("win", A_, cs(256, 256))
            cp(cq[2], proj_fm(w1, 0, 128, nTt, TT))
            cp(ckv[0], proj_fm(w1, 128, 128, nTt, TT))
            w2 = wload("win", A_, cs(512, 160))
            cp(ckv[1], proj_fm(w2, 0, 128, nTt, TT))
            pA = proj_fm(w2, 64, 96, nTt, TT)
            w3 = wload("wkrot", A_, A_)
            pB = proj_fm(w3, 0, 96, nTt, TT)
            cqn_t = [scr.get() for _ in range(2)]
            cqn = [cqn_t[i // 2].bf()[:, (i % 2) * TT:(i % 2 + 1) * TT] for i in range(3)]
            rmsnorm(cq, [vcol(V_QN + i) for i in range(3)], 384, cqn, TT)
            ckvn_t = scr.get()
            ckvn = [ckvn_t.bf()[:, i * TT:(i + 1) * TT] for i in range(2)]
            rmsnorm(ckv, [vcol(V_KVN + i) for i in range(2)], 256, ckvn, TT)
            rope_combine(pA, pB, [kT[h][t][64:96, :] for h in range(8)])
            scr.put(*cq)
            scr.put(*ckv)

            gq_tile, gk_tile = scr.get(), scr.get()
            gq_b = [gq_tile.bf()[:, pr * TT:(pr + 1) * TT] for pr in range(2)]
            gk_b = [gk_tile.bf()[:, pr * TT:(pr + 1) * TT] for pr in range(2)]
            wg0 = wload("win", A_, cs(C_GQ, 256))
            for pr in range(2):
                cp(gq_b[pr], proj_fm(wg0, pr * 128, 128, nTt, TT))
            wg1 = wload("win", A_, cs(C_GK, 256))
            for pr in range(2):
                cp(gk_b[pr], proj_fm(wg1, pr * 128, 128, nTt, TT), eng=dve)
            gktm_tile = scr.get()
            gktm = [gktm_tile.bf()[:, i * 256:(i + 1) * 256] for i in range(4)]
            for i in range(4):
                p = ps_get()
                for k in range(8):
                    mm(p[:, 0:256], nTt[k][:, i * 128:(i + 1) * 128], wg1[:, k, :], k == 0, k == 7)
                cp(gktm[i], p[:, 0:256], eng=(dve if i % 2 else act))
            gv_t = [scr.get() for _ in range(2)]
            wv0 = wload("win", A_, cs(C_GV, 256))
            wv1 = wload("win", A_, cs(C_GV + 256, 256))
            for i in range(4):
                p = ps_get()
                for hv, wv_ in enumerate((wv0, wv1)):
                    for k in range(8):
                        mm(p[:, hv * 256:(hv + 1) * 256], nTt[k][:, i * 128:(i + 1) * 128], wv_[:, k, :], k == 0, k == 7)
                cp(gv_t[i // 2].bf()[:, (i % 2) * 512:(i % 2 + 1) * 512], p, eng=(dve if i % 2 else act))
            wga = wload("win", A_, cs(C_GA - 112, 128))
            p = proj_fm(wga, 112, 16, nTt, TT)
            cp(gaug[0:16, :], p[0:16, :])
            la_t = [scr.get() for _ in range(2)]
            for i in range(4):
                p = ps_get()
                mm(p[:, 0:256], gaug[:, i * 128:(i + 1) * 128], wa2_t, True, True)
                la = la_t[i // 2][:, (i % 2) * 256:(i % 2 + 1) * 256]
                actf(la, p[:, 0:256], AF.Exp, scale=-1.0)
                actf(la, la, AF.Ln, bias=ones_f[:, 0:1])
                ts(la, la, -1.0 / 16.0, None, ALU.mult)
            gr_t = [scr.get() for _ in range(2)]
            for hh in range(4):
                if hh % 2 == 0:
                    wgr = wload("win", A_, cs(C_GR + hh * 128, 256))
                p = proj_fm(wgr, (hh % 2) * 128, 128, nTt, TT)
                actf(gr_t[hh // 2].bf()[:, (hh % 2) * TT:(hh % 2 + 1) * TT], p, AF.Silu)

            qtT_tile = scr.get()
            qtT_all = [qtT_tile.bf()[:, pr * TT:(pr + 1) * TT] for pr in range(2)]
            la_c = [la_t[i // 2][:, (i % 2) * 256:(i % 2 + 1) * 256] for i in range(4)]
            for pr in range(2):
                pb = ps_get()
                for i in range(4):
                    mm(pb[:, i * 128:(i + 1) * 128], la_c[i][:, pr * 128:(pr + 1) * 128], utri, True, True)
                Et = scr.get()
                actf(Et, pb, AF.Exp)
                stt(qtT_all[pr], gq_b[pr], 0.125, Et, ALU.mult, ALU.mult)
                cp(gdec[:, pr * 4:(pr + 1) * 4], Et[:, :].re("p (i t) -> p i t", t=128)[:, :, 127], eng=dve)
                Eit = scr.get()
                actf(Eit, pb, AF.Exp, scale=-1.0)
                for o2 in range(2):
                    o = o2 * 64
                    tt(ktz_all[pr * 2 + o2][o:o + 64, :], gk_b[pr][o:o + 64, :], Eit[o:o + 64, :], ALU.mult)
                scr.put(Et, Eit)
            kdec_tile = scr.get()
            kdec_all = kdec_tile.bf()
            for half in range(2):
                pq = ps_get()
                for i2 in range(2):
                    mm(pq[:, i2 * 256:(i2 + 1) * 256], sltri, la_c[half * 2 + i2], True, True)
                sft = scr.get()
                actf(sft, pq, AF.Exp)
                tt(kdec_all[:, half * 512:(half + 1) * 512], gktm_tile.bf()[:, half * 512:(half + 1) * 512], sft, ALU.mult)
                scr.put(sft)
            scr.put(gq_tile, gk_tile, gktm_tile)
            scr.put(*la_t)
            attm_tiles = [scr.get(), scr.get()]
            attm_all = [attm_tiles[i // 2].bf()[:, (i % 2) * 512:(i % 2 + 1) * 512] for i in range(4)]
            for i in range(4):
                csl = slice(i * 128, (i + 1) * 128)
                pa = ps_get()
                for h4 in range(4):
                    mm(pa[:, h4 * 128:(h4 + 1) * 128], ktz_all[h4][:, csl], qtT_all[h4 // 2][:, csl], True, True)
                tt(attm_all[i], pa, utri4[:, :, :].re("p a b -> p (a b)"), ALU.mult)

            xq_tiles = [scr.get(), scr.get()]
            xq_all = [xq_tiles[hh // 2].bf()[:, (hh % 2) * TT:(hh % 2 + 1) * TT] for hh in range(4)]
            for half in range(2):
                wxq = wload("win", A_, cs(C_XQ + half * 256, 256))
                for h2 in range(2):
                    cp(xq_all[half * 2 + h2], proj_fm(wxq, h2 * 128, 128, nTt, TT), eng=(dve if h2 else act))

            wkv = wload("wukv", A_, A_)
            for h in range(8):
                p = ps_get()
                for k in range(2):
                    mm(p[0:64, :], wkv[:, k, h * 128:h * 128 + 64], ckvn[k], k == 0, k == 1)
                cp(kT[h][t][0:64, :], p[0:64, :], eng=(dve if h % 2 else act))
            for i in range(4):
                p = ps_get()
                for k in range(2):
                    rhs = wkv[:, k, :].re("p (h two d) -> p h two d", two=2, d=64)[:, :, 1, :]
                    mm(p[:, :].re("p (h d) -> p h d", h=8), ckvn[k][:, i * 128:(i + 1) * 128], rhs, k == 0, k == 1)
                cp(vaug[4 * t + i][:, :, 0:64], p[:, :].re("p (h d) -> p h d", h=8))
            scr.put(ckvn_t)

            qt_tiles = [scr.get() for _ in range(4)]
            qT = [qt_tiles[h // 2].bf()[:, (h % 2) * TT:(h % 2 + 1) * TT] for h in range(8)]
            for qt_ in qt_tiles:
                memset(qt_.bf()[96:128, :], 0.0, eng=pool)
            for half in range(2):
                wq = wload("wuq", A_, cs(half * 384, 384))
                wqr = wload("wuqr", A_, cs(half * 384, 384))
                for hh in range(4):
                    h = half * 4 + hh
                    pA = proj_fm(wq, hh * 96, 96, cqn, TT, nk=3)
                    pB = proj_fm(wqr, hh * 96, 96, cqn, TT, nk=3)
                    cp(qT[h][0:64, :], pA[0:64, :])
                    rope_combine(pA, pB, [qT[h][64:96, :]])
            scr.put(*cqn_t)
            scr.put(Ct, St)

            om_tiles = [scr.get() for _ in range(2)]
            oT_mla = [om_tiles[i // 2].bf()[:, (i % 2) * TT:(i % 2 + 1) * TT] for i in range(4)]
            nk = 4 * t + 4
            sc = 1.0 / math.sqrt(96.0)
            pend_norm = [None]

            def flush_norm():
                if pend_norm[0] is not None:
                    pend_norm[0]()
                    pend_norm[0] = None

            ps_held.add(6)
            ps_held.add(7)
            for h in range(8):
                po = psum[6 + (h % 2)]
                kTh, qTh = kT[h], qT[h]
                nchunk = [0]

                def chunk_done():
                    nchunk[0] += 1
                    if nchunk[0] == 3:
                        flush_norm()

                def score_into(dst, kc, c0, masked=False):
                    kt128 = R(kTh[kc // 4], B_h[:, h * S + kc * 128: h * S + (kc + 1) * 128])
                    mm(dst[:, c0:TT], kt128, qTh[:, c0:TT], True, not masked)
                    if masked:
                        mm(dst[:, c0:c0 + 128], ident_bf, negm_bf, False, True)

                def pv(kc, rhs, c0):
                    w128 = R(vaug[kc], vaug_flat[:, (kc * 8 + h) * 65:(kc * 8 + h) * 65 + 128])
                    extra = [vaug[kc + 1]] if (h >= 7 and kc + 1 < 16) else ([vaug_pad] if h >= 7 else [])
                    f.op(pe, lambda e: e.matmul(po.ap[:, c0:TT], lhsT=w128.ap, rhs=_ap(rhs), start=(kc == 0),
                                                stop=(kc == nk - 1)),
                         outs=[po], ins=[w128, rhs] + extra, inc=(kc == nk - 1))

                npair = 2 * t
                pairs = []

                def issue_pair(j):
                    pa_, pb_, ap2 = ps_pair()
                    score_into(pa_, 2 * j, 0)
                    score_into(pb_, 2 * j + 1, 0)
                    pairs.append((pa_, pb_, ap2))

                for j in range(min(1, npair)):
                    issue_pair(j)
                diag_q = []

                def issue_diag(jd):
                    kc = 4 * t + jd
                    pst = ps_get()
                    score_into(pst, kc, jd * 128, masked=True)
                    diag_q.append(pst)

                if npair < 1:
                    issue_diag(0)
                nd_issued = len(diag_q)
                for j in range(npair):
                    pa_, pb_, ap2 = pairs.pop(0)
                    if j + 1 < npair:
                        issue_pair(j + 1)
                    else:
                        issue_diag(0)
                        issue_diag(1)
                        nd_issued = 2
                    PT = scr.get()
                    pt2 = PT.bf()
                    f.op(act, lambda e: e.activation(out=pt2.ap.rearrange("p (a b) -> p a b", a=2), in_=ap2,
                                                     func=AF.Exp, scale=sc), outs=[PT], ins=[pa_, pb_])
                    pv(2 * j, pt2[:, 0:TT], 0)
                    pv(2 * j + 1, pt2[:, TT:2 * TT], 0)
                    scr.put(PT)
                    chunk_done()
                    chunk_done()
                for jd in range(4):
                    kc = 4 * t + jd
                    c0 = jd * 128
                    while nd_issued < min(4, jd + 2):
                        issue_diag(nd_issued)
                        nd_issued += 1
                    pst = diag_q.pop(0)
                    PT = scr.get()
                    actf(PT.bf()[:, c0:TT], pst[:, c0:TT], AF.Exp, scale=sc)
                    pv(kc, PT.bf()[:, c0:TT], c0)
                    scr.put(PT)
                    chunk_done()
                flush_norm()
                den = scr.get()
                cp(den[0:1, :], po[64:65, :])
                pd = ps_get(hold=True)
                mm(pd[0:64, :], ones_f[0:1, 0:64], den[0:1, :], True, True)

                def norm_b(po=po, pd=pd, den=den, h=h):
                    rd = scr.get()
                    actf(rd[0:64, :], pd[0:64, :], AF.Ln)
                    actf(rd[0:64, :], rd[0:64, :], AF.Exp, scale=-1.0)
                    tt(oT_mla[h // 2][(h % 2) * 64:(h % 2) * 64 + 64, :], po[0:64, :], rd[0:64, :], ALU.mult)
                    ps_rel(pd)
                    scr.put(den, rd)

                pend_norm[0] = norm_b
            flush_norm()
            ps_held.discard(6)
            ps_held.discard(7)
            scr.put(*qt_tiles)
            if stop_after == "m_mla":
                return

            og_tiles = [scr.get() for _ in range(4)]
            ox_tiles = [scr.get() for _ in range(2)]
            oT_x = [ox_tiles[i // 2].bf()[:, (i % 2) * TT:(i % 2 + 1) * TT] for i in range(4)]
            scx = 1.0 / math.sqrt(128.0)
            wxq_box = [None]

            for i in range(4):
                hh = i
                csl = slice(i * 128, (i + 1) * 128)
                xq = xq_all[hh]
                attm = attm_all[i]
                xpo = ps_get(hold=True)
                xpden = ps_get(hold=True)
                xPT = []
                for mc in range(2):
                    pst = ps_get()
                    mm(pst, kxT[:, hh, mc * 128:(mc + 1) * 128], xq, True, True)
                    PT = scr.get()
                    actf(PT.bf()[:, 0:TT], pst, AF.Exp, scale=scx)
                    xPT.append(PT)
                for mc in range(2):
                    mm(xpo, vx[:, mc, hh * 128:(hh + 1) * 128], xPT[mc].bf()[:, 0:TT], mc == 0, mc == 1)
                    mm(xpden, ones_bf, xPT[mc].bf()[:, 0:TT], mc == 0, mc == 1)
                scr.put(*xPT)
                po = ps_get()
                gv = gv_t[i // 2].bf()[:, (i % 2) * 512:(i % 2 + 1) * 512]
                for h4 in range(4):
                    pr = h4 // 2
                    mm(po[:, h4 * 128:(h4 + 1) * 128], gv[:, h4 * 128:(h4 + 1) * 128],
                       attm[:, h4 * 128:(h4 + 1) * 128], True, False)
                    mm(po[:, h4 * 128:(h4 + 1) * 128], gstate_bf[h4], qtT_all[pr][:, csl], False, True)
                for pr in range(2):
                    pds = ps_get()
                    mm(pds[:, 0:256], kdec_all[:, i * 256 + pr * 128:i * 256 + (pr + 1) * 128],
                       gv[:, pr * 256:(pr + 1) * 256], True, True)
                    for o2 in range(2):
                        o = o2 * 64
                        dcol = gdec[o:o + 64, pr * 4 + i:pr * 4 + i + 1]
                        dS = pds[o:o + 64, o2 * 128:(o2 + 1) * 128]
                        stt(gstate_bf[pr * 2 + o2][o:o + 64, :], gstate[o:o + 64, pr, :], dcol, dS, ALU.mult, ALU.add)
                        stt(gstate[o:o + 64, pr, :], gstate[o:o + 64, pr, :], dcol, dS, ALU.mult, ALU.add)
                for h4 in range(4):
                    cp(og_tiles[h4][:, csl], po[:, h4 * 128:(h4 + 1) * 128], eng=(dve if h4 % 2 else act))
                rd = scr.get()
                actf(rd, xpden, AF.Ln)
                actf(rd, rd, AF.Exp, scale=-1.0)
                tt(oT_x[hh], xpo, rd, ALU.mult)
                ps_rel(xpo)
                ps_rel(xpden)
                scr.put(rd)
            scr.put(qtT_tile, kdec_tile)
            scr.put(*attm_tiles)
            scr.put(*xq_tiles)
            scr.put(*gv_t)
            ogl_tiles = [scr.get() for _ in range(2)]
            oT_gla = [ogl_tiles[i // 2].bf()[:, (i % 2) * TT:(i % 2 + 1) * TT] for i in range(4)]
            for hh in range(4):
                tmp = scr.get()
                rmsnorm([og_tiles[hh]], [vcol(V_ON)], 128, [tmp], TT)
                tt(oT_gla[hh], tmp, gr_t[hh // 2].bf()[:, (hh % 2) * TT:(hh % 2 + 1) * TT], ALU.mult)
                scr.put(tmp)
            scr.put(*og_tiles)
            scr.put(*gr_t)
            if stop_after == "m_x":
                return

            mg_tiles = [scr.get() for _ in range(4)]
            mT = [mg_tiles[c // 2].bf()[:, (c % 2) * TT:(c % 2 + 1) * TT] for c in range(8)]
            wonames = ["wo_mla", "wo_gla", "wo_x"]
            oTs = [oT_mla, oT_gla, oT_x]
            for cg in range(4):
                accs = [scr.get(), scr.get()]
                for b in range(3):
                    wz = wload("win", A_, cs(C_GZ + b * D + cg * 256, 256))
                    wo = wload(wonames[b], A_, cs(cg * 256, 256))
                    for c2 in range(2):
                        c = cg * 2 + c2
                        acc = accs[c2]
                        pz = proj_fm(wz, c2 * 128, 128, nTt, TT)
                        g = scr.get()
                        actf(g, pz, AF.Sigmoid, bias=vcol(V_GB + b * 8 + c))
                        pp = ps_get()
                        for k in range(4):
                            mm(pp, wo[:, k, c2 * 128:(c2 + 1) * 128], oTs[b][k], k == 0, k == 3)
                        if b == 0:
                            tt(acc, pp, g, ALU.mult)
                        else:
                            tt(g, pp, g, ALU.mult)
                            tt(mT[c] if b == 2 else acc, acc, g, ALU.add)
                        scr.put(g)
                scr.put(*accs)
            scr.put(*om_tiles)
            scr.put(*ogl_tiles)
            scr.put(*ox_tiles)
            scr.put(*nt_tiles)
            for cg in range(4):
                wo_ = wload("wout", A_, cs(cg * 256, 256))
                for c2 in range(2):
                    c = cg * 2 + c2
                    py = ps_get()
                    for k in range(8):
                        mm(py, wo_[:, k, c2 * 128:(c2 + 1) * 128], mT[k], k == 0, k == 7)
                    tt(hT[c][t], py, hT[c][t], ALU.add)
            scr.put(*mg_tiles)
            if t + 1 < NT:
                normed = make_norm(t + 1)
                tables = make_tables(t + 1)
        f.barrier()

    final = stop_after is None

    def emit_out(t):
        outs = [scr.get() for _ in range(8)]
        if final:
            rmsnorm([hT[c][t] for c in range(8)], [vcol(V_FIN + c) for c in range(8)], D, outs, TT)
        for c in range(8):
            src = outs[c] if final else hT[c][t]
            f.dma(sp, outT_d[c * 128:(c + 1) * 128, t * TT:(t + 1) * TT], src, is_output=True)
        scr.put(*outs)

    ffn("f1g", "f1u", "f1d", V_F1)
    if stop_after != "ffn1":
        mixer()
        if stop_after is None:
            ffn("f2g", "f2u", "f2d", V_F2, on_tile_done=emit_out, end_barrier=False)
    if not final:
        for t in range(NT):
            emit_out(t)
    f.finish()
    f.close()
    return nc, {e.name: (e.ninstr, e.nwait) for e in (pe, act, dve, pool, sp)}, wl_rec


def _prep_shared(inp):
    f32 = np.float32
    g = lambda k: np.ascontiguousarray(np.asarray(inp[k], dtype=f32))
    w_in = g("w_in")[0]
    vecs = np.zeros((128, NV), f32)

    def put(col, v):
        v = np.asarray(v, f32).reshape(-1)
        n = v.size // 128
        vecs[:, col:col + n] = v.reshape(n, 128).T

    put(V_F1, g("ffn1_norm")[0])
    put(V_MIX, g("mix_norm")[0])
    put(V_F2, g("ffn2_norm")[0])
    put(V_FIN, g("final_norm"))
    put(V_MEM, g("mem_norm")[0])
    gb = g("gate_bias")[0]
    for b in range(3):
        put(V_GB + 8 * b, gb[b])
    put(V_QN, g("mla_q_norm")[0])
    put(V_KVN, g("mla_kv_norm")[0])
    put(V_ON, g("gla_o_norm")[0])
    half = 16
    inv_freq = (10000.0 ** (-np.arange(half, dtype=np.float32) / half)).astype(f32)
    vecs[64:96, V_IF] = np.concatenate([inv_freq, inv_freq])
    vecs[96:128, V_IF] = np.concatenate([inv_freq, inv_freq])
    vecs[64:96, V_SG] = np.concatenate([-np.ones(16, f32), np.ones(16, f32)])
    jj = np.arange(128)[:, None]
    tq = np.arange(128)[None, :]
    utri = (jj <= tq).astype(f32)
    sltri = (jj > tq).astype(f32)
    ident = (jj == tq).astype(f32)
    wuq = g("mla_w_uq")[0]
    perm = []
    for h in range(8):
        b = h * 96
        perm += list(range(b, b + 64)) + list(range(b + 80, b + 96)) + list(range(b + 64, b + 80))
    wuqr = np.ascontiguousarray(wuq[:, perm])
    wkrot = np.ascontiguousarray(np.concatenate([w_in[:, 576:640], w_in[:, 656:672], w_in[:, 640:656]], axis=1))
    wa2 = np.zeros((32, 256), f32)
    wa2[0:16] = g("gla_w_a2")[0]
    wa2[16] = g("gla_b_a")[0]
    return {
        "vecs": vecs, "utri": utri, "sltri": sltri, "ident": ident,
        "f1g": g("ffn1_wg")[0], "f1u": g("ffn1_wu")[0], "f1d": g("ffn1_wd")[0],
        "f2g": g("ffn2_wg")[0], "f2u": g("ffn2_wu")[0], "f2d": g("ffn2_wd")[0],
        "win": w_in, "wkrot": wkrot, "wuq": wuq, "wuqr": wuqr, "wukv": g("mla_w_ukv")[0], "wa2": wa2,
        "wo_mla": g("mla_w_o")[0], "wo_gla": g("gla_w_o")[0], "wo_x": g("x_w_o")[0],
        "xwkv": g("x_w_kv")[0], "wout": g("w_out")[0],
    }


_CACHE = {}


def kernel(stop_after=None, n_cores=8, **inp):
    key = stop_after
    if key not in _CACHE:
        _, _, rec = build_program(stop_after)
        _CACHE[key] = build_program(stop_after, wl_seq=rec)
    nc, stats, _ = _CACHE[key]
    shared = _prep_shared(inp)
    x = np.asarray(inp["x"], np.float32)
    mem = np.asarray(inp["mem"], np.float32)
    pos = np.asarray(inp["positions"], np.int32)
    in_maps = []
    for b in range(n_cores):
        m = dict(shared)
        m["xT"] = np.ascontiguousarray(x[b].T)
        m["memT"] = np.ascontiguousarray(mem[b].T)
        m["posr"] = np.ascontiguousarray(np.broadcast_to(pos[b][None, :], (32, S)))
        in_maps.append(m)
    res = run_bass_kernel_spmd(nc, in_maps, core_ids=list(range(n_cores)))
    out = np.stack([np.ascontiguousarray(r["outT"].T) for r in res.results], axis=0)
    return out.astype(np.float32)
```

```python
import math
import numpy as np
import concourse.bass as bass
import concourse.mybir as mybir
from concourse.bass_utils import run_bass_kernel_spmd

F32 = mybir.dt.float32
BF16 = mybir.dt.bfloat16
I32 = mybir.dt.int32
AF = mybir.ActivationFunctionType
ALU = mybir.AluOpType

D = 1024
S = 2048
NMEM = 256
DFF = 2816
DIN = 5808
EPS = 1e-6
TT = 512
NT = S // TT


class T:
    def __init__(self, ap, name=""):
        self.ap = ap
        self.name = name
        self.w = None
        self.r = []

    def __getitem__(self, idx):
        return R(self, self.ap[idx])

    def bf(self):
        return R(self, self.ap.bitcast(BF16))

    def i32(self):
        return R(self, self.ap.bitcast(I32))


class R:
    def __init__(self, t, ap):
        self.t = t
        self.ap = ap

    def __getitem__(self, idx):
        return R(self.t, self.ap[idx])

    def re(self, s, **kw):
        return R(self.t, self.ap.rearrange(s, **kw))


def _ap(x):
    return x.ap if isinstance(x, (T, R)) else x


def _t(x):
    if isinstance(x, T):
        return x
    if isinstance(x, R):
        return x.t
    return None


class Eng:
    def __init__(self, name, h, semidx, sem):
        self.name = name
        self.h = h
        self.semidx = semidx
        self.sem = sem
        self.count = 0
        self.known = {}
        self.ninstr = 0
        self.nwait = 0


class FW:
    def __init__(self, nc, n_dma_sems=48):
        self.nc = nc
        self.sems = []
        self._stack = []
        self.pe = self._mk("pe", nc.tensor)
        self.act = self._mk("act", nc.scalar)
        self.dve = self._mk("dve", nc.vector)
        self.pool = self._mk("pool", nc.gpsimd)
        self.sp = self._mk("sp", nc.sync)
        self.dma_sems = []
        for i in range(n_dma_sems):
            idx = self._new_sem(f"dma{i}")
            self.dma_sems.append([idx, 0])
        self.dma_rr = 0
        self.out_events = []

    def _new_sem(self, name):
        cm = self.nc.semaphore(name)
        s = cm.__enter__()
        self._stack.append(cm)
        self.sems.append(s)
        return len(self.sems) - 1

    def _mk(self, name, h):
        idx = self._new_sem("s_" + name)
        return Eng(name, h, idx, self.sems[idx])

    def close(self):
        for cm in reversed(self._stack):
            cm.__exit__(None, None, None)
        self._stack = []

    def sb(self, name, shape, dtype):
        cm = self.nc.sbuf_tensor(name, list(shape), dtype)
        h = cm.__enter__()
        self._stack.append(cm)
        return h

    def ps(self, name, shape, dtype=F32):
        cm = self.nc.psum_tensor(name, list(shape), dtype)
        h = cm.__enter__()
        self._stack.append(cm)
        return h

    def _wait(self, eng, events):
        need = {}
        for ev in events:
            if ev is None:
                continue
            s, v, _ = ev
            if v > need.get(s, 0):
                need[s] = v
        for s, v in need.items():
            if eng.known.get(s, 0) >= v:
                continue
            eng.h.wait_ge(self.sems[s], v)
            eng.known[s] = v
            eng.nwait += 1

    def _deps(self, eng, outs, ins):
        evs = []
        for x in ins:
            t = _t(x)
            if t is None or t.w is None:
                continue
            if t.w[2] == eng.name and eng.name == "pe":
                continue
            evs.append(t.w)
        for x in outs:
            t = _t(x)
            if t is None:
                continue
            if t.w is not None and t.w[2] != eng.name:
                evs.append(t.w)
            for ev in t.r:
                if ev[2] != eng.name:
                    evs.append(ev)
        return evs

    def op(self, eng, fn, outs=(), ins=(), inc=True):
        self._wait(eng, self._deps(eng, outs, ins))
        instr = fn(eng.h)
        eng.ninstr += 1
        if inc:
            instr.then_inc(eng.sem, 1)
            eng.count += 1
            ev = (eng.semidx, eng.count, eng.name)
        else:
            ev = (eng.semidx, eng.count + 1, eng.name)
        for x in ins:
            t = _t(x)
            if t is not None:
                t.r.append(ev)
                if len(t.r) > 64:
                    t.r = t.r[-48:]
        for x in outs:
            t = _t(x)
            if t is not None:
                t.w = ev
                t.r = []
        return instr

    def dma(self, q, out, in_, is_output=False, **kw):
        outs = [out] if _t(out) is not None else []
        ins = [in_] if _t(in_) is not None else []
        evs = []
        for x in ins:
            t = _t(x)
            if t.w is not None:
                evs.append(t.w)
        for x in outs:
            t = _t(x)
            if t.w is not None:
                evs.append(t.w)
            evs.extend(t.r)
        slot = self.dma_sems[self.dma_rr % len(self.dma_sems)]
        self.dma_rr += 1
        if slot[1] > 0:
            evs.append((slot[0], slot[1], "dma"))
        self._wait(q, evs)
        instr = q.h.dma_start(out=_ap(out), in_=_ap(in_), **kw)
        q.ninstr += 1
        slot[1] += 16
        instr.then_inc(self.sems[slot[0]], 16)
        ev = (slot[0], slot[1], "dma")
        for x in ins:
            _t(x).r.append(ev)
        for x in outs:
            t = _t(x)
            t.w = ev
            t.r = []
        if is_output:
            self.out_events.append(ev)
        return ev

    def barrier(self):
        engs = (self.pe, self.act, self.dve)
        for e in engs:
            evs = [(o.semidx, o.count, o.name) for o in engs if o is not e and o.count > 0]
            self._wait(e, evs)

    def finish(self):
        evs = list(self.out_events)
        for slot in self.dma_sems:
            if slot[1] > 0:
                evs.append((slot[0], slot[1], "dma"))
        self._wait(self.sp, evs)


class Pool_:
    def __init__(self, tiles):
        self.free = list(tiles)
        self.n = len(tiles)

    def get(self):
        assert self.free, "scratch pool exhausted"
        self.low = min(getattr(self, "low", 999), len(self.free) - 1)
        return self.free.pop(0)

    def put(self, *ts):
        for t in ts:
            assert t not in self.free
            self.free.append(t)


C_CQ, C_CKV, C_KR, C_GQ, C_GK, C_GV, C_GA, C_GR, C_XQ, C_GZ = 0, 384, 640, 672, 928, 1184, 1696, 1712, 2224, 2736
V_F1, V_MIX, V_F2, V_FIN, V_MEM, V_GB, V_QN, V_KVN, V_ON, V_IF, V_SG = 0, 8, 16, 24, 32, 40, 64, 67, 69, 70, 71
NV = 72


LOOKAHEAD = 3


def build_program(stop_after=None, wl_seq=None):
    nc = bass.Bass("TRN2", target_bir_lowering=False)
    wl_rec = []

    def din(name, shape, dt=F32):
        return nc.dram_tensor(name, list(shape), dt, kind="ExternalInput").ap()

    xT_d = din("xT", [D, S])
    memT_d = din("memT", [D, NMEM])
    posr_d = din("posr", [32, S], I32)
    vecs_d = din("vecs", [128, NV])
    utri_d = din("utri", [128, 128])
    sltri_d = din("sltri", [128, 128])
    ident_d = din("ident", [128, 128])
    f1g_d, f1u_d, f1d_d = din("f1g", [D, DFF]), din("f1u", [D, DFF]), din("f1d", [DFF, D])
    f2g_d, f2u_d, f2d_d = din("f2g", [D, DFF]), din("f2u", [D, DFF]), din("f2d", [DFF, D])
    win_d = din("win", [D, DIN])
    wkrot_d = din("wkrot", [D, 96])
    wuq_d = din("wuq", [384, 768])
    wuqr_d = din("wuqr", [384, 768])
    wukv_d = din("wukv", [256, 1024])
    wa2_d = din("wa2", [32, 256])
    wo_mla_d, wo_gla_d, wo_x_d = din("wo_mla", [512, D]), din("wo_gla", [512, D]), din("wo_x", [512, D])
    xwkv_d = din("xwkv", [D, D])
    wout_d = din("wout", [D, D])
    outT_d = nc.dram_tensor("outT", [D, S], F32, kind="ExternalOutput").ap()

    f = FW(nc)
    pe, act, dve, pool, sp = f.pe, f.act, f.dve, f.pool, f.sp

    hT_h = f.sb("hT", [128, 8, S], F32)
    hT = [[T(hT_h[:, c, t * TT:(t + 1) * TT], f"hT{c}_{t}") for t in range(NT)] for c in range(8)]
    B_h = f.sb("B", [128, 8 * S], BF16)
    vaug_flat = f.sb("vaug", [128, 16 * 8 * 65 + 64], BF16)
    vaug_h = vaug_flat[:, 0:16 * 8 * 65].rearrange("p (k h d) -> p k h d", k=16, h=8)
    vaug = [T(vaug_h[:, kc], f"vaug{kc}") for kc in range(16)]
    vaug_pad = T(vaug_flat[:, 16 * 8 * 65:], "vaug_pad")
    NSLOT = 6
    ring = [T(f.sb(f"ring{i}", [128, 2048], BF16)[:], f"ring{i}") for i in range(NSLOT)]
    ring_pos = [0]
    NSCR = 27
    scr = Pool_([T(f.sb(f"scr{i}", [128, 512], F32)[:], f"scr{i}") for i in range(NSCR)])
    ps_all = f.ps("ps_all", [128, 8, 512], F32)
    psum = [T(ps_all[:, i, :], f"ps{i}") for i in range(8)]
    pair_pos = [0]
    ps_pos = [0]

    ps_held = set()
    ps_last = [0] * 8
    ps_clock = [0]

    def _stamp(i):
        ps_clock[0] += 1
        ps_last[i] = ps_clock[0]

    def ps_get(hold=False):
        cand = [i for i in range(8) if i not in ps_held]
        i = min(cand, key=lambda b: ps_last[b])
        _stamp(i)
        if hold:
            ps_held.add(i)
        return psum[i]

    def ps_rel(p):
        ps_held.discard(psum.index(p))

    def ps_take(i):
        ps_held.add(i)
        _stamp(i)
        return psum[i]

    def ps_pair():
        cand = [j for j in range(4) if 2 * j not in ps_held and 2 * j + 1 not in ps_held]
        j = min(cand, key=lambda q: max(ps_last[2 * q], ps_last[2 * q + 1]))
        _stamp(2 * j)
        _stamp(2 * j + 1)
        return psum[2 * j], psum[2 * j + 1], ps_all[:, 2 * j:2 * j + 2, :]

    vecs = T(f.sb("vecs_s", [128, NV], F32)[:], "vecs")
    utri = T(f.sb("utri_s", [128, 128], F32)[:], "utri")
    sltri = T(f.sb("sltri_s", [128, 128], F32)[:], "sltri")
    utri4 = T(f.sb("utri4", [128, 4, 128], BF16)[:], "utri4")
    ident_bf = T(f.sb("ident_bf", [128, 128], BF16)[:], "ident_bf")
    negm_bf = T(f.sb("negm_bf", [128, 128], BF16)[:], "negm_bf")
    ones_bf = T(f.sb("ones_bf", [128, 128], BF16)[:], "ones_bf")
    ones_f = T(f.sb("ones_f", [128, 128], F32)[:], "ones_f")
    eps_t = T(f.sb("eps", [128, 1], F32)[:], "eps")
    wa2_t = T(f.sb("wa2_s", [32, 256], F32)[:], "wa2")
    gaug = T(f.sb("gaug", [32, TT], F32)[:], "gaug")
    gstate = T(f.sb("gstate", [128, 2, 128], F32)[:], "gstate")
    gstate_bf = [T(f.sb(f"gstate_bf{i}", [128, 128], BF16)[:], f"gstate_bf{i}") for i in range(4)]
    ktz_all = [T(f.sb(f"ktz{i}", [128, TT], BF16)[:], f"ktz{i}") for i in range(4)]
    gdec = T(f.sb("gdec", [128, 8], F32)[:], "gdec")
    kxT = T(f.sb("kxT", [128, 4, NMEM], BF16)[:], "kxT")
    vx = T(f.sb("vx", [128, 2, 512], BF16)[:], "vx")

    def mm(out, lhsT, rhs, start, stop):
        f.op(pe, lambda e: e.matmul(_ap(out), lhsT=_ap(lhsT), rhs=_ap(rhs), start=start, stop=stop),
             outs=[out], ins=[lhsT, rhs], inc=stop)

    def actf(out, in_, func, scale=1.0, bias=None):
        ins = [in_]
        kw = {}
        if bias is not None:
            kw["bias"] = _ap(bias)
            ins.append(bias)
        if isinstance(scale, (T, R)):
            ins.append(scale)
        f.op(act, lambda e: e.activation(out=_ap(out), in_=_ap(in_), func=func, scale=_ap(scale), **kw),
             outs=[out], ins=ins)

    def tt(out, a, b, op, eng=None):
        f.op(eng or dve, lambda e: e.tensor_tensor(out=_ap(out), in0=_ap(a), in1=_ap(b), op=op),
             outs=[out], ins=[a, b])

    def stt(out, in0, scalar, in1, op0, op1):
        ins = [in0, in1] + ([scalar] if isinstance(scalar, (T, R)) else [])
        f.op(dve, lambda e: e.scalar_tensor_tensor(out=_ap(out), in0=_ap(in0), scalar=_ap(scalar), in1=_ap(in1),
                                                   op0=op0, op1=op1), outs=[out], ins=ins)

    def ts(out, in0, s1, s2, op0, op1=None, eng=None):
        ins = [in0] + [s for s in (s1, s2) if isinstance(s, (T, R))]
        if op1 is None:
            f.op(eng or dve, lambda e: e.tensor_scalar(out=_ap(out), in0=_ap(in0), scalar1=_ap(s1), scalar2=None,
                                                       op0=op0), outs=[out], ins=ins)
        else:
            f.op(eng or dve, lambda e: e.tensor_scalar(out=_ap(out), in0=_ap(in0), scalar1=_ap(s1), scalar2=_ap(s2),
                                                       op0=op0, op1=op1), outs=[out], ins=ins)

    def cp(out, in_, eng=None):
        eng = eng or act
        if eng is act:
            f.op(act, lambda e: e.activation(out=_ap(out), in_=_ap(in_), func=AF.Copy), outs=[out], ins=[in_])
        else:
            f.op(eng, lambda e: e.tensor_copy(out=_ap(out), in_=_ap(in_)), outs=[out], ins=[in_])

    def memset(x, val, eng=None):
        f.op(eng or dve, lambda e: e.memset(_ap(x), val), outs=[x])

    wviews = {}
    issued = {}

    def _issue(i, name, idx):
        dram_ap = wviews[name][(slice(None),) + tuple(idx)]
        slot = ring[i % NSLOT]
        shp = dram_ap.shape
        n = int(np.prod(shp[1:]))
        assert n <= 2048, shp
        dst = slot.ap[:, 0:n].rearrange("p (a b) -> p a b", a=shp[1])
        v = R(slot, dst)
        f.dma(pool, v, dram_ap)
        issued[i] = v

    def wload(name, *idx):
        i = ring_pos[0]
        ring_pos[0] += 1
        if wl_seq is None:
            wl_rec.append((name, idx))
            _issue(i, name, idx)
        else:
            assert wl_seq[i] == (name, idx), (i, wl_seq[i], name, idx)
            for j in range(i, min(i + LOOKAHEAD + 1, len(wl_seq))):
                if j not in issued:
                    _issue(j, *wl_seq[j])
        return issued.pop(i)

    kcp = lambda d: d.rearrange("(kc p) f -> p kc f", p=128)
    for nm, d in (("f1g", f1g_d), ("f1u", f1u_d), ("f1d", f1d_d), ("f2g", f2g_d), ("f2u", f2u_d), ("f2d", f2d_d),
                  ("win", win_d), ("wkrot", wkrot_d), ("wuq", wuq_d), ("wuqr", wuqr_d), ("wukv", wukv_d),
                  ("wo_mla", wo_mla_d), ("wo_gla", wo_gla_d), ("wo_x", wo_x_d), ("xwkv", xwkv_d), ("wout", wout_d)):
        wviews[nm] = kcp(d)
    A_ = slice(None)

    def cs(a, n):
        return slice(a, a + n)

    f.dma(sp, vecs, vecs_d)
    f.dma(sp, utri, utri_d)
    f.dma(sp, sltri, sltri_d)
    f.dma(sp, wa2_t, wa2_d)
    for half in range(2):
        for c in range(8):
            dst = R(hT[c][2 * half], hT_h[:, c, half * 1024:(half + 1) * 1024])
            ev = f.dma(sp, dst, xT_d[c * 128:(c + 1) * 128, half * 1024:(half + 1) * 1024])
            hT[c][2 * half + 1].w = ev
    memset(ones_bf, 1.0)
    memset(ones_f, 1.0)
    memset(eps_t, EPS)
    memset(gaug, 1.0)
    memset(gstate, 0.0)
    for i in range(4):
        memset(gstate_bf[i], 0.0)
        memset(ktz_all[i], 0.0)
    for kc in range(16):
        memset(vaug[kc][:, :, 64:65], 1.0, eng=pool)
    memset(vaug_pad, 0.0, eng=pool)
    for h4 in range(4):
        cp(utri4[:, h4, :], utri, eng=dve)
    f.dma(pool, ident_bf, ident_d)
    ts(negm_bf, utri, -1.0, 30000.0, ALU.add, ALU.mult)

    def vcol(c):
        return vecs[:, c:c + 1]

    def rmsnorm(srcs, gcols, dim, outs, n, extra=None):
        p = ps_get()
        for i, s in enumerate(srcs):
            sq = scr.get()
            actf(sq.bf()[:, 0:n], s, AF.Square)
            mm(p[:, 0:n], ones_bf, sq.bf()[:, 0:n], i == 0, i == len(srcs) - 1)
            scr.put(sq)
        rstd = scr.get()
        actf(rstd[:, 0:n], p[:, 0:n], AF.Ln, scale=1.0 / dim, bias=eps_t)
        actf(rstd[:, 0:n], rstd[:, 0:n], AF.Exp, scale=-0.5)
        for i, s in enumerate(srcs):
            stt(outs[i], s, gcols[i], rstd[:, 0:n], ALU.mult, ALU.mult)
        scr.put(rstd)

    def ffn(ng, nu, nd, vbase, on_tile_done=None, end_barrier=True):
        nT = [[T(B_h[:, c * S + t * TT: c * S + (t + 1) * TT], f"nT{c}_{t}") for t in range(NT)] for c in range(8)]
        for t in range(NT):
            rmsnorm([hT[c][t] for c in range(8)], [vcol(vbase + c) for c in range(8)], D,
                    [nT[c][t] for c in range(8)], TT)
        NG = DFF // 256

        def loads(g):
            return (wload(ng, A_, cs(g * 256, 256)), wload(nu, A_, cs(g * 256, 256)), wload(nd, cs(2 * g, 2), A_))

        pending = [None]

        def down(wd, a_tiles, t, last=False):
            for c in range(8):
                py = ps_get()
                for jj in range(2):
                    mm(py, wd[:, jj, c * 128:(c + 1) * 128], a_tiles[jj].bf()[:, 0:TT], jj == 0, jj == 1)
                stt(hT[c][t], py, 0.5, hT[c][t], ALU.mult, ALU.add)
            scr.put(*a_tiles)
            if last and on_tile_done is not None:
                on_tile_done(t)

        for g in range(NG):
            if pending[0] is not None:
                pending[0]()
                pending[0] = None
            wg, wu, wd = loads(g)
            for t in range(NT):
                a_tiles = []
                for jj in range(2):
                    pg = ps_get()
                    pu = ps_get()
                    for k in range(8):
                        mm(pg, wg[:, k, jj * 128:(jj + 1) * 128], nT[k][t], k == 0, k == 7)
                    for k in range(8):
                        mm(pu, wu[:, k, jj * 128:(jj + 1) * 128], nT[k][t], k == 0, k == 7)
                    s = scr.get()
                    actf(s, pg, AF.Silu)
                    a = scr.get()
                    tt(a.bf()[:, 0:TT], s, pu, ALU.mult)
                    scr.put(s)
                    a_tiles.append(a)
                if pending[0] is not None:
                    pending[0]()
                pending[0] = (lambda wd=wd, a_tiles=a_tiles, t=t, g=g: down(wd, a_tiles, t, last=(g == NG - 1)))
        pending[0]()
        if end_barrier:
            f.barrier()

    def mixer():
        kT = [[T(B_h[0:96, h * S + t * TT: h * S + (t + 1) * TT], f"kT{h}_{t}") for t in range(NT)] for h in range(8)]

        def proj_fm(wslot, col0, ncols, src, n, nk=8):
            p = ps_get()
            for k in range(nk):
                mm(p[0:ncols, 0:n], wslot[:, k, col0:col0 + ncols], src[k], k == 0, k == nk - 1)
            return p

        memf = [scr.get() for _ in range(4)]
        memv = memT_d.rearrange("(kc p) m -> p kc m", p=128)
        for i in range(4):
            f.dma(pool, memf[i][:, 0:512].re("p (a b) -> p a b", a=2), memv[:, 2 * i:2 * i + 2, :])
        memn = [scr.get() for _ in range(2)]
        msrc = [memf[c // 2][:, (c % 2) * 256:(c % 2 + 1) * 256] for c in range(8)]
        mdst = [memn[c // 4].bf()[:, (c % 4) * 256:(c % 4 + 1) * 256] for c in range(8)]
        rmsnorm(msrc, [vcol(V_MEM + c) for c in range(8)], D, mdst, NMEM)
        scr.put(*memf)
        for hh in range(4):
            if hh % 2 == 0:
                wk = wload("xwkv", A_, cs(hh * 128, 256))
            p = ps_get()
            for k in range(8):
                mm(p[:, 0:NMEM], wk[:, k, (hh % 2) * 128:(hh % 2 + 1) * 128], mdst[k], k == 0, k == 7)
            cp(kxT[:, hh, :], p[:, 0:NMEM])
        for half in range(2):
            wv = wload("xwkv", A_, cs(512 + half * 256, 256))
            for mc in range(2):
                p = ps_get()
                for k in range(8):
                    mm(p[:, 0:256], mdst[k][:, mc * 128:(mc + 1) * 128], wv[:, k, :], k == 0, k == 7)
                cp(vx[:, mc, half * 256:(half + 1) * 256], p[:, 0:256])
        scr.put(*memn)

        TWO_PI = 2.0 * math.pi
        C1 = 6.28125
        C2 = TWO_PI - C1
        utri_bf = utri4[:, 0, :]

        def make_tables(t):
            tsl = slice(t * TT, (t + 1) * TT)
            Ct, St = scr.get(), scr.get()
            pos_i = scr.get()
            f.dma(sp, pos_i.i32()[64:96, :], posr_d[:, tsl])
            f.dma(sp, pos_i.i32()[96:128, :], posr_d[:, tsl])
            a2, kf, m = scr.get(), scr.get(), scr.get()
            cp(a2[64:128, :], pos_i.i32()[64:128, :], eng=dve)
            ts(a2[64:128, :], a2[64:128, :], vecs[64:128, V_IF:V_IF + 1], None, ALU.mult)
            ts(a2[96:128, :], a2[96:128, :], math.pi / 2, None, ALU.add)
            ts(kf[64:128, :], a2[64:128, :], 1.0 / TWO_PI, None, ALU.mult)
            cp(m.i32()[64:128, :], kf[64:128, :], eng=dve)
            cp(kf[64:128, :], m.i32()[64:128, :], eng=dve)
            stt(a2[64:128, :], kf[64:128, :], -C1, a2[64:128, :], ALU.mult, ALU.add)
            stt(a2[64:128, :], kf[64:128, :], -C2, a2[64:128, :], ALU.mult, ALU.add)
            ts(m[64:128, :], a2[64:128, :], math.pi, TWO_PI, ALU.is_gt, ALU.mult)
            tt(a2[64:128, :], a2[64:128, :], m[64:128, :], ALU.subtract)
            ts(m[64:128, :], a2[64:128, :], -math.pi, TWO_PI, ALU.is_lt, ALU.mult)
            tt(a2[64:128, :], a2[64:128, :], m[64:128, :], ALU.add)
            ts(a2[64:128, :], a2[64:128, :], 3.1415925, -3.1415925, ALU.min, ALU.max)
            actf(St[64:96, :], a2[64:96, :], AF.Sin)
            actf(Ct[64:96, :], a2[96:128, :], AF.Sin)
            ts(St[64:96, :], St[64:96, :], vecs[64:96, V_SG:V_SG + 1], None, ALU.mult)
            scr.put(pos_i, a2, kf, m)
            return Ct, St

        def make_norm(t):
            nt_tiles = [scr.get() for _ in range(4)]
            nTt = [nt_tiles[c // 2].bf()[:, (c % 2) * TT:(c % 2 + 1) * TT] for c in range(8)]
            rmsnorm([hT[c][t] for c in range(8)], [vcol(V_MIX + c) for c in range(8)], D, nTt, TT)
            return nt_tiles, nTt

        normed = make_norm(0)
        tables = make_tables(0)
        for t in range(NT):
            nt_tiles, nTt = normed
            Ct, St = tables

            def rope_combine(pA, pB, outs):
                t1, t2 = scr.get(), scr.get()
                tt(t1[64:96, :], pA[64:96, :], Ct[64:96, :], ALU.mult)
                tt(t2[64:96, :], pB[64:96, :], St[64:96, :], ALU.mult)
                for o in outs:
                    tt(o, t1[64:96, :], t2[64:96, :], ALU.add)
                scr.put(t1, t2)

            cq = [scr.get() for _ in range(3)]
            ckv = [scr.get() for _ in range(2)]
            w0 = wload("win", A_, cs(0, 256))
            cp(cq[0], proj_fm(w0, 0, 128, nTt, TT))
            cp(cq[1], proj_fm(w0, 128, 128, nTt, TT))
            w1 = wload("win", A_, cs(256, 256))
            cp(cq[2], proj_fm(w1, 0, 128, nTt, TT))
            cp(ckv[0], proj_fm(w1, 128, 128, nTt, TT))
            w2 = wload("win", A_, cs(512, 160))
            cp(ckv[1], proj_fm(w2, 0, 128, nTt, TT))
            pA = proj_fm(w2, 64, 96, nTt, TT)
            w3 = wload("wkrot", A_, A_)
            pB = proj_fm(w3, 0, 96, nTt, TT)
            cqn_t = [scr.get() for _ in range(2)]
            cqn = [cqn_t[i // 2].bf()[:, (i % 2) * TT:(i % 2 + 1) * TT] for i in range(3)]
            rmsnorm(cq, [vcol(V_QN + i) for i in range(3)], 384, cqn, TT)
            ckvn_t = scr.get()
            ckvn = [ckvn_t.bf()[:, i * TT:(i + 1) * TT] for i in range(2)]
            rmsnorm(ckv, [vcol(V_KVN + i) for i in range(2)], 256, ckvn, TT)
            rope_combine(pA, pB, [kT[h][t][64:96, :] for h in range(8)])
            scr.put(*cq)
            scr.put(*ckv)

            gq_tile, gk_tile = scr.get(), scr.get()
            gq_b = [gq_tile.bf()[:, pr * TT:(pr + 1) * TT] for pr in range(2)]
            gk_b = [gk_tile.bf()[:, pr * TT:(pr + 1) * TT] for pr in range(2)]
            wg0 = wload("win", A_, cs(C_GQ, 256))
            for pr in range(2):
                cp(gq_b[pr], proj_fm(wg0, pr * 128, 128, nTt, TT))
            wg1 = wload("win", A_, cs(C_GK, 256))
            for pr in range(2):
                cp(gk_b[pr], proj_fm(wg1, pr * 128, 128, nTt, TT), eng=dve)
            gktm_tile = scr.get()
            gktm = [gktm_tile.bf()[:, i * 256:(i + 1) * 256] for i in range(4)]
            for i in range(4):
                p = ps_get()
                for k in range(8):
                    mm(p[:, 0:256], nTt[k][:, i * 128:(i + 1) * 128], wg1[:, k, :], k == 0, k == 7)
                cp(gktm[i], p[:, 0:256], eng=(dve if i % 2 else act))
            gv_t = [scr.get() for _ in range(2)]
            wv0 = wload("win", A_, cs(C_GV, 256))
            wv1 = wload("win", A_, cs(C_GV + 256, 256))
            for i in range(4):
                p = ps_get()
                for hv, wv_ in enumerate((wv0, wv1)):
                    for k in range(8):
                        mm(p[:, hv * 256:(hv + 1) * 256], nTt[k][:, i * 128:(i + 1) * 128], wv_[:, k, :], k == 0, k == 7)
                cp(gv_t[i // 2].bf()[:, (i % 2) * 512:(i % 2 + 1) * 512], p, eng=(dve if i % 2 else act))
            wga = wload("win", A_, cs(C_GA - 112, 128))
            p = proj_fm(wga, 112, 16, nTt, TT)
            cp(gaug[0:16, :], p[0:16, :])
            la_t = [scr.get() for _ in range(2)]
            for i in range(4):
                p = ps_get()
                mm(p[:, 0:256], gaug[:, i * 128:(i + 1) * 128], wa2_t, True, True)
                la = la_t[i // 2][:, (i % 2) * 256:(i % 2 + 1) * 256]
                actf(la, p[:, 0:256], AF.Exp, scale=-1.0)
                actf(la, la, AF.Ln, bias=ones_f[:, 0:1])
                ts(la, la, -1.0 / 16.0, None, ALU.mult)
            gr_t = [scr.get() for _ in range(2)]
            for hh in range(4):
                if hh % 2 == 0:
                    wgr = wload("win", A_, cs(C_GR + hh * 128, 256))
                p = proj_fm(wgr, (hh % 2) * 128, 128, nTt, TT)
                actf(gr_t[hh // 2].bf()[:, (hh % 2) * TT:(hh % 2 + 1) * TT], p, AF.Silu)

            qtT_tile = scr.get()
            qtT_all = [qtT_tile.bf()[:, pr * TT:(pr + 1) * TT] for pr in range(2)]
            la_c = [la_t[i // 2][:, (i % 2) * 256:(i % 2 + 1) * 256] for i in range(4)]
            for pr in range(2):
                pb = ps_get()
                for i in range(4):
                    mm(pb[:, i * 128:(i + 1) * 128], la_c[i][:, pr * 128:(pr + 1) * 128], utri, True, True)
                Et = scr.get()
                actf(Et, pb, AF.Exp)
                stt(qtT_all[pr], gq_b[pr], 0.125, Et, ALU.mult, ALU.mult)
                cp(gdec[:, pr * 4:(pr + 1) * 4], Et[:, :].re("p (i t) -> p i t", t=128)[:, :, 127], eng=dve)
                Eit = scr.get()
                actf(Eit, pb, AF.Exp, scale=-1.0)
                for o2 in range(2):
                    o = o2 * 64
                    tt(ktz_all[pr * 2 + o2][o:o + 64, :], gk_b[pr][o:o + 64, :], Eit[o:o + 64, :], ALU.mult)
                scr.put(Et, Eit)
            kdec_tile = scr.get()
            kdec_all = kdec_tile.bf()
            for half in range(2):
                pq = ps_get()
                for i2 in range(2):
                    mm(pq[:, i2 * 256:(i2 + 1) * 256], sltri, la_c[half * 2 + i2], True, True)
                sft = scr.get()
                actf(sft, pq, AF.Exp)
                tt(kdec_all[:, half * 512:(half + 1) * 512], gktm_tile.bf()[:, half * 512:(half + 1) * 512], sft, ALU.mult)
                scr.put(sft)
            scr.put(gq_tile, gk_tile, gktm_tile)
            scr.put(*la_t)

            xq_tiles = [scr.get(), scr.get()]
            xq_all = [xq_tiles[hh // 2].bf()[:, (hh % 2) * TT:(hh % 2 + 1) * TT] for hh in range(4)]
            for half in range(2):
                wxq = wload("win", A_, cs(C_XQ + half * 256, 256))
                for h2 in range(2):
                    cp(xq_all[half * 2 + h2], proj_fm(wxq, h2 * 128, 128, nTt, TT), eng=(dve if h2 else act))

            wkv = wload("wukv", A_, A_)
            for h in range(8):
                p = ps_get()
                for k in range(2):
                    mm(p[0:64, :], wkv[:, k, h * 128:h * 128 + 64], ckvn[k], k == 0, k == 1)
                cp(kT[h][t][0:64, :], p[0:64, :], eng=(dve if h % 2 else act))
            for i in range(4):
                p = ps_get()
                for k in range(2):
                    rhs = wkv[:, k, :].re("p (h two d) -> p h two d", two=2, d=64)[:, :, 1, :]
                    mm(p[:, :].re("p (h d) -> p h d", h=8), ckvn[k][:, i * 128:(i + 1) * 128], rhs, k == 0, k == 1)
                cp(vaug[4 * t + i][:, :, 0:64], p[:, :].re("p (h d) -> p h d", h=8))
            scr.put(ckvn_t)

            qt_tiles = [scr.get() for _ in range(4)]
            qT = [qt_tiles[h // 2].bf()[:, (h % 2) * TT:(h % 2 + 1) * TT] for h in range(8)]
            for qt_ in qt_tiles:
                memset(qt_.bf()[96:128, :], 0.0, eng=pool)
            for half in range(2):
                wq = wload("wuq", A_, cs(half * 384, 384))
                wqr = wload("wuqr", A_, cs(half * 384, 384))
                for hh in range(4):
                    h = half * 4 + hh
                    pA = proj_fm(wq, hh * 96, 96, cqn, TT, nk=3)
                    pB = proj_fm(wqr, hh * 96, 96, cqn, TT, nk=3)
                    cp(qT[h][0:64, :], pA[0:64, :])
                    rope_combine(pA, pB, [qT[h][64:96, :]])
            scr.put(*cqn_t)
            scr.put(Ct, St)

            om_tiles = [scr.get() for _ in range(2)]
            oT_mla = [om_tiles[i // 2].bf()[:, (i % 2) * TT:(i % 2 + 1) * TT] for i in range(4)]
            nk = 4 * t + 4
            sc = 1.0 / math.sqrt(96.0)
            pend_norm = [None]

            def flush_norm():
                if pend_norm[0] is not None:
                    pend_norm[0]()
                    pend_norm[0] = None

            ps_held.add(6)
            ps_held.add(7)
            for h in range(8):
                po = psum[6 + (h % 2)]
                kTh, qTh = kT[h], qT[h]
                nchunk = [0]

                def chunk_done():
                    nchunk[0] += 1
                    if nchunk[0] == 3:
                        flush_norm()

                def score_into(dst, kc, c0, masked=False):
                    kt128 = R(kTh[kc // 4], B_h[:, h * S + kc * 128: h * S + (kc + 1) * 128])
                    mm(dst[:, c0:TT], kt128, qTh[:, c0:TT], True, not masked)
                    if masked:
                        mm(dst[:, c0:c0 + 128], ident_bf, negm_bf, False, True)

                def pv(kc, rhs, c0):
                    w128 = R(vaug[kc], vaug_flat[:, (kc * 8 + h) * 65:(kc * 8 + h) * 65 + 128])
                    extra = [vaug[kc + 1]] if (h >= 7 and kc + 1 < 16) else ([vaug_pad] if h >= 7 else [])
                    f.op(pe, lambda e: e.matmul(po.ap[:, c0:TT], lhsT=w128.ap, rhs=_ap(rhs), start=(kc == 0),
                                                stop=(kc == nk - 1)),
                         outs=[po], ins=[w128, rhs] + extra, inc=(kc == nk - 1))

                npair = 2 * t
                pairs = []

                def issue_pair(j):
                    pa_, pb_, ap2 = ps_pair()
                    score_into(pa_, 2 * j, 0)
                    score_into(pb_, 2 * j + 1, 0)
                    pairs.append((pa_, pb_, ap2))

                for j in range(min(1, npair)):
                    issue_pair(j)
                diag_q = []

                def issue_diag(jd):
                    kc = 4 * t + jd
                    pst = ps_get()
                    score_into(pst, kc, jd * 128, masked=True)
                    diag_q.append(pst)

                if npair < 1:
                    issue_diag(0)
                nd_issued = len(diag_q)
                for j in range(npair):
                    pa_, pb_, ap2 = pairs.pop(0)
                    if j + 1 < npair:
                        issue_pair(j + 1)
                    else:
                        issue_diag(0)
                        issue_diag(1)
                        nd_issued = 2
                    PT = scr.get()
                    pt2 = PT.bf()
                    f.op(act, lambda e: e.activation(out=pt2.ap.rearrange("p (a b) -> p a b", a=2), in_=ap2,
                                                     func=AF.Exp, scale=sc), outs=[PT], ins=[pa_, pb_])
                    pv(2 * j, pt2[:, 0:TT], 0)
                    pv(2 * j + 1, pt2[:, TT:2 * TT], 0)
                    scr.put(PT)
                    chunk_done()
                    chunk_done()
                for jd in range(4):
                    kc = 4 * t + jd
                    c0 = jd * 128
                    while nd_issued < min(4, jd + 2):
                        issue_diag(nd_issued)
                        nd_issued += 1
                    pst = diag_q.pop(0)
                    PT = scr.get()
                    actf(PT.bf()[:, c0:TT], pst[:, c0:TT], AF.Exp, scale=sc)
                    pv(kc, PT.bf()[:, c0:TT], c0)
                    scr.put(PT)
                    chunk_done()
                flush_norm()
                den = scr.get()
                cp(den[0:1, :], po[64:65, :])
                pd = ps_get(hold=True)
                mm(pd[0:64, :], ones_f[0:1, 0:64], den[0:1, :], True, True)

                def norm_b(po=po, pd=pd, den=den, h=h):
                    rd = scr.get()
                    actf(rd[0:64, :], pd[0:64, :], AF.Ln)
                    actf(rd[0:64, :], rd[0:64, :], AF.Exp, scale=-1.0)
                    tt(oT_mla[h // 2][(h % 2) * 64:(h % 2) * 64 + 64, :], po[0:64, :], rd[0:64, :], ALU.mult)
                    ps_rel(pd)
                    scr.put(den, rd)

                pend_norm[0] = norm_b
            flush_norm()
            ps_held.discard(6)
            ps_held.discard(7)
            scr.put(*qt_tiles)
            if stop_after == "m_mla":
                return

            og_tiles = [scr.get() for _ in range(4)]
            ox_tiles = [scr.get() for _ in range(2)]
            oT_x = [ox_tiles[i // 2].bf()[:, (i % 2) * TT:(i % 2 + 1) * TT] for i in range(4)]
            scx = 1.0 / math.sqrt(128.0)
            wxq_box = [None]

            for i in range(4):
                hh = i
                csl = slice(i * 128, (i + 1) * 128)
                xq = xq_all[hh]
                qk = scr.get()
                xpo = ps_get(hold=True)
                xpden = ps_get(hold=True)
                xPT = []
                for mc in range(2):
                    pst = ps_get()
                    mm(pst, kxT[:, hh, mc * 128:(mc + 1) * 128], xq, True, True)
                    PT = scr.get()
                    actf(PT.bf()[:, 0:TT], pst, AF.Exp, scale=scx)
                    xPT.append(PT)
                pa = ps_get()
                for h4 in range(4):
                    pr = h4 // 2
                    mm(pa[:, h4 * 128:(h4 + 1) * 128], ktz_all[h4][:, csl], qtT_all[pr][:, csl], True, True)
                attm = qk.bf()[:, 0:512]
                tt(attm, pa, utri4[:, :, :].re("p a b -> p (a b)"), ALU.mult)
                for mc in range(2):
                    mm(xpo, vx[:, mc, hh * 128:(hh + 1) * 128], xPT[mc].bf()[:, 0:TT], mc == 0, mc == 1)
                    mm(xpden, ones_bf, xPT[mc].bf()[:, 0:TT], mc == 0, mc == 1)
                scr.put(*xPT)
                gv = gv_t[i // 2].bf()[:, (i % 2) * 512:(i % 2 + 1) * 512]
                pds = ps_get()
                for pr in range(2):
                    mm(pds[:, pr * 256:(pr + 1) * 256], kdec_all[:, i * 256 + pr * 128:i * 256 + (pr + 1) * 128],
                       gv[:, pr * 256:(pr + 1) * 256], True, True)
                po = ps_get()
                for h4 in range(4):
                    pr = h4 // 2
                    mm(po[:, h4 * 128:(h4 + 1) * 128], gv[:, h4 * 128:(h4 + 1) * 128],
                       attm[:, h4 * 128:(h4 + 1) * 128], True, False)
                    mm(po[:, h4 * 128:(h4 + 1) * 128], gstate_bf[h4], qtT_all[pr][:, csl], False, True)
                upd = []
                for pr in range(2):
                    for o2 in range(2):
                        o = o2 * 64
                        upd.append((pr, o2, o, gdec[o:o + 64, pr * 4 + i:pr * 4 + i + 1],
                                    pds[o:o + 64, pr * 256 + o2 * 128:pr * 256 + (o2 + 1) * 128]))
                for (pr, o2, o, dcol, dS) in upd:
                    stt(gstate_bf[pr * 2 + o2][o:o + 64, :], gstate[o:o + 64, pr, :], dcol, dS, ALU.mult, ALU.add)
                for (pr, o2, o, dcol, dS) in upd:
                    stt(gstate[o:o + 64, pr, :], gstate[o:o + 64, pr, :], dcol, dS, ALU.mult, ALU.add)
                for h4 in range(4):
                    cp(og_tiles[h4][:, csl], po[:, h4 * 128:(h4 + 1) * 128], eng=(dve if h4 % 2 else act))
                scr.put(qk)
                rd = scr.get()
                actf(rd, xpden, AF.Ln)
                actf(rd, rd, AF.Exp, scale=-1.0)
                tt(oT_x[hh], xpo, rd, ALU.mult)
                ps_rel(xpo)
                ps_rel(xpden)
                scr.put(rd)
            scr.put(qtT_tile, kdec_tile)
            scr.put(*xq_tiles)
            scr.put(*gv_t)
            ogl_tiles = [scr.get() for _ in range(2)]
            oT_gla = [ogl_tiles[i // 2].bf()[:, (i % 2) * TT:(i % 2 + 1) * TT] for i in range(4)]
            for hh in range(4):
                tmp = scr.get()
                rmsnorm([og_tiles[hh]], [vcol(V_ON)], 128, [tmp], TT)
                tt(oT_gla[hh], tmp, gr_t[hh // 2].bf()[:, (hh % 2) * TT:(hh % 2 + 1) * TT], ALU.mult)
                scr.put(tmp)
            scr.put(*og_tiles)
            scr.put(*gr_t)
            if stop_after == "m_x":
                return

            mg_tiles = [scr.get() for _ in range(4)]
            mT = [mg_tiles[c // 2].bf()[:, (c % 2) * TT:(c % 2 + 1) * TT] for c in range(8)]
            wonames = ["wo_mla", "wo_gla", "wo_x"]
            oTs = [oT_mla, oT_gla, oT_x]
            for cg in range(4):
                accs = [scr.get(), scr.get()]
                for b in range(3):
                    wz = wload("win", A_, cs(C_GZ + b * D + cg * 256, 256))
                    wo = wload(wonames[b], A_, cs(cg * 256, 256))
                    for c2 in range(2):
                        c = cg * 2 + c2
                        acc = accs[c2]
                        pz = proj_fm(wz, c2 * 128, 128, nTt, TT)
                        g = scr.get()
                        actf(g, pz, AF.Sigmoid, bias=vcol(V_GB + b * 8 + c))
                        pp = ps_get()
                        for k in range(4):
                            mm(pp, wo[:, k, c2 * 128:(c2 + 1) * 128], oTs[b][k], k == 0, k == 3)
                        if b == 0:
                            tt(acc, pp, g, ALU.mult)
                        else:
                            tt(g, pp, g, ALU.mult)
                            tt(mT[c] if b == 2 else acc, acc, g, ALU.add)
                        scr.put(g)
                scr.put(*accs)
            scr.put(*om_tiles)
            scr.put(*ogl_tiles)
            scr.put(*ox_tiles)
            scr.put(*nt_tiles)
            for cg in range(4):
                wo_ = wload("wout", A_, cs(cg * 256, 256))
                for c2 in range(2):
                    c = cg * 2 + c2
                    py = ps_get()
                    for k in range(8):
                        mm(py, wo_[:, k, c2 * 128:(c2 + 1) * 128], mT[k], k == 0, k == 7)
                    tt(hT[c][t], py, hT[c][t], ALU.add)
            scr.put(*mg_tiles)
            if t + 1 < NT:
                normed = make_norm(t + 1)
                tables = make_tables(t + 1)
        f.barrier()

    final = stop_after is None

    def emit_out(t):
        outs = [scr.get() for _ in range(8)]
        if final:
            rmsnorm([hT[c][t] for c in range(8)], [vcol(V_FIN + c) for c in range(8)], D, outs, TT)
        for c in range(8):
            src = outs[c] if final else hT[c][t]
            f.dma(sp, outT_d[c * 128:(c + 1) * 128, t * TT:(t + 1) * TT], src, is_output=True)
        scr.put(*outs)

    ffn("f1g", "f1u", "f1d", V_F1)
    if stop_after != "ffn1":
        mixer()
        if stop_after is None:
            ffn("f2g", "f2u", "f2d", V_F2, on_tile_done=emit_out, end_barrier=False)
    if not final:
        for t in range(NT):
            emit_out(t)
    f.finish()
    f.close()
    return nc, {e.name: (e.ninstr, e.nwait) for e in (pe, act, dve, pool, sp)}, wl_rec


def _prep_shared(inp):
    f32 = np.float32
    g = lambda k: np.ascontiguousarray(np.asarray(inp[k], dtype=f32))
    w_in = g("w_in")[0]
    vecs = np.zeros((128, NV), f32)

    def put(col, v):
        v = np.asarray(v, f32).reshape(-1)
        n = v.size // 128
        vecs[:, col:col + n] = v.reshape(n, 128).T

    put(V_F1, g("ffn1_norm")[0])
    put(V_MIX, g("mix_norm")[0])
    put(V_F2, g("ffn2_norm")[0])
    put(V_FIN, g("final_norm"))
    put(V_MEM, g("mem_norm")[0])
    gb = g("gate_bias")[0]
    for b in range(3):
        put(V_GB + 8 * b, gb[b])
    put(V_QN, g("mla_q_norm")[0])
    put(V_KVN, g("mla_kv_norm")[0])
    put(V_ON, g("gla_o_norm")[0])
    half = 16
    inv_freq = (10000.0 ** (-np.arange(half, dtype=np.float32) / half)).astype(f32)
    vecs[64:96, V_IF] = np.concatenate([inv_freq, inv_freq])
    vecs[96:128, V_IF] = np.concatenate([inv_freq, inv_freq])
    vecs[64:96, V_SG] = np.concatenate([-np.ones(16, f32), np.ones(16, f32)])
    jj = np.arange(128)[:, None]
    tq = np.arange(128)[None, :]
    utri = (jj <= tq).astype(f32)
    sltri = (jj > tq).astype(f32)
    ident = (jj == tq).astype(f32)
    wuq = g("mla_w_uq")[0]
    perm = []
    for h in range(8):
        b = h * 96
        perm += list(range(b, b + 64)) + list(range(b + 80, b + 96)) + list(range(b + 64, b + 80))
    wuqr = np.ascontiguousarray(wuq[:, perm])
    wkrot = np.ascontiguousarray(np.concatenate([w_in[:, 576:640], w_in[:, 656:672], w_in[:, 640:656]], axis=1))
    wa2 = np.zeros((32, 256), f32)
    wa2[0:16] = g("gla_w_a2")[0]
    wa2[16] = g("gla_b_a")[0]
    return {
        "vecs": vecs, "utri": utri, "sltri": sltri, "ident": ident,
        "f1g": g("ffn1_wg")[0], "f1u": g("ffn1_wu")[0], "f1d": g("ffn1_wd")[0],
        "f2g": g("ffn2_wg")[0], "f2u": g("ffn2_wu")[0], "f2d": g("ffn2_wd")[0],
        "win": w_in, "wkrot": wkrot, "wuq": wuq, "wuqr": wuqr, "wukv": g("mla_w_ukv")[0], "wa2": wa2,
        "wo_mla": g("mla_w_o")[0], "wo_gla": g("gla_w_o")[0], "wo_x": g("x_w_o")[0],
        "xwkv": g("x_w_kv")[0], "wout": g("w_out")[0],
    }


_CACHE = {}


def kernel(stop_after=None, n_cores=8, **inp):
    key = stop_after
    if key not in _CACHE:
        _, _, rec = build_program(stop_after)
        _CACHE[key] = build_program(stop_after, wl_seq=rec)
    nc, stats, _ = _CACHE[key]
    shared = _prep_shared(inp)
    x = np.asarray(inp["x"], np.float32)
    mem = np.asarray(inp["mem"], np.float32)
    pos = np.asarray(inp["positions"], np.int32)
    in_maps = []
    for b in range(n_cores):
        m = dict(shared)
        m["xT"] = np.ascontiguousarray(x[b].T)
        m["memT"] = np.ascontiguousarray(mem[b].T)
        m["posr"] = np.ascontiguousarray(np.broadcast_to(pos[b][None, :], (32, S)))
        in_maps.append(m)
    res = run_bass_kernel_spmd(nc, in_maps, core_ids=list(range(n_cores)))
    out = np.stack([np.ascontiguousarray(r["outT"].T) for r in res.results], axis=0)
    return out.astype(np.float32)
```
